# Optimizing a Trainium2 kernel written in Bass

```python
import math
import jax, jax.numpy as jnp
from jax import lax
import numpy as np

D_MODEL = 1024
BATCH = 8
SEQ = 2048
DEPTH = 2

EPS = 1e-6
CONV_CH = D_MODEL
CONV_K = 31
CONV_PAD = (CONV_K - 1) // 2
ATTN_HEADS = 8
ATTN_DH = 64
ATTN_QK = ATTN_HEADS * 2 * ATTN_DH
ATTN_V = ATTN_HEADS * 2 * ATTN_DH
Q_BLOCK = 128
SGU_WIDTH = D_MODEL
SGU_GROUPS = 8
SGU_GROUP_DIM = SGU_WIDTH // SGU_GROUPS
CHUNK = 128
N_BRANCH = 3
D_FF = 4 * D_MODEL
COL_SIZES = (CONV_CH, CONV_CH, ATTN_QK, ATTN_QK, ATTN_V, SGU_WIDTH, SGU_WIDTH, N_BRANCH * D_MODEL)
SPLITS = tuple(int(s) for s in np.cumsum(COL_SIZES)[:-1])
W_IN_COLS = int(sum(COL_SIZES))

kernel_name = "hybrid_conv_diffattn_sgu_encoder"


def rms_norm(x, g):
    xf = x.astype(jnp.float32)
    y = xf * lax.rsqrt(jnp.mean(xf * xf, axis=-1, keepdims=True) + EPS)
    return (y * g.astype(jnp.float32)).astype(x.dtype)


def layer_norm(x, g, b):
    xf = x.astype(jnp.float32)
    mu = jnp.mean(xf, axis=-1, keepdims=True)
    xc = xf - mu
    y = xc * lax.rsqrt(jnp.mean(xc * xc, axis=-1, keepdims=True) + EPS)
    return (y * g.astype(jnp.float32) + b.astype(jnp.float32)).astype(x.dtype)


def alibi_slopes(n_heads):
    return jnp.exp2(-8.0 * jnp.arange(1, n_heads + 1, dtype=jnp.float32) / n_heads)


def depthwise_conv(x, w, b):
    c = x.shape[-1]
    y = lax.conv_general_dilated(
        x, w[:, None, :].astype(x.dtype), window_strides=(1,),
        padding=[(CONV_PAD, CONV_PAD)],
        dimension_numbers=("NWC", "WIO", "NWC"), feature_group_count=c)
    return y + b.astype(x.dtype)


def diff_attention(q, k, v, lam, slopes):
    b, s = q.shape[0], q.shape[1]
    nb = s // Q_BLOCK
    scale = ATTN_DH ** -0.5
    pos = jnp.arange(s, dtype=jnp.float32)
    qb = q.reshape(b, nb, Q_BLOCK, ATTN_HEADS, 2, ATTN_DH).transpose(1, 0, 2, 3, 4, 5)
    qpos = pos.reshape(nb, Q_BLOCK)
    kf = k.astype(jnp.float32)
    vf = v.astype(jnp.float32)

    def block(args):
        qblk, tq = args
        sc = jnp.einsum("bqhjd,bkhjd->bhjqk", qblk.astype(jnp.float32), kf) * scale
        dist = jnp.abs(tq[:, None] - pos[None, :])
        sc = sc - slopes[None, :, None, None, None] * dist[None, None, None]
        p = jax.nn.softmax(sc, axis=-1)
        a = p[:, :, 0] - lam * p[:, :, 1]
        return jnp.einsum("bhqk,bkhe->bqhe", a, vf)

    o = lax.map(block, (qb, qpos))
    return o.transpose(1, 0, 2, 3, 4).reshape(b, s, ATTN_HEADS, 2 * ATTN_DH).astype(q.dtype)


def spatial_gating(u, v, w_s, b_s):
    b, s, _ = v.shape
    n = s // CHUNK
    vc = v.reshape(b, n, CHUNK, SGU_GROUPS, SGU_GROUP_DIM)
    mixed = jnp.einsum("gts,bnsgc->bntgc", w_s.astype(v.dtype), vc) + b_s.T.astype(v.dtype)[None, None, :, :, None]
    return u * mixed.reshape(b, s, SGU_WIDTH)


def setup_inputs(seed: int = 0) -> dict:
    key = jax.random.key(seed)
    ks = jax.random.split(key, 32)
    f32 = jnp.float32
    L, D = DEPTH, D_MODEL

    def nrm(k, shape, scale):
        return jax.random.normal(k, shape, f32) * scale

    def gain(k, shape):
        return 1.0 + 0.02 * jax.random.normal(k, shape, f32)

    return {
        "x": jax.random.normal(ks[0], (BATCH, SEQ, D), f32),
        "norm_mix_pre": gain(ks[1], (L, D)),
        "norm_mix_post": gain(ks[2], (L, D)),
        "w_in": nrm(ks[3], (L, D, W_IN_COLS), D ** -0.5),
        "b_gate": nrm(ks[4], (L, N_BRANCH * D), 0.02),
        "conv_w": nrm(ks[5], (L, CONV_K, CONV_CH), CONV_K ** -0.5),
        "conv_b": nrm(ks[6], (L, CONV_CH), 0.02),
        "conv_ln_g": gain(ks[7], (L, CONV_CH)),
        "conv_ln_b": nrm(ks[8], (L, CONV_CH), 0.02),
        "lam_q1": nrm(ks[9], (L, ATTN_DH), 0.1),
        "lam_k1": nrm(ks[10], (L, ATTN_DH), 0.1),
        "lam_q2": nrm(ks[11], (L, ATTN_DH), 0.1),
        "lam_k2": nrm(ks[12], (L, ATTN_DH), 0.1),
        "subln_g": gain(ks[13], (L, 2 * ATTN_DH)),
        "sgu_ln_g": gain(ks[14], (L, SGU_WIDTH)),
        "sgu_ln_b": nrm(ks[15], (L, SGU_WIDTH), 0.02),
        "sgu_w": nrm(ks[16], (L, SGU_GROUPS, CHUNK, CHUNK), CHUNK ** -0.5),
        "sgu_b": gain(ks[17], (L, SGU_GROUPS, CHUNK)),
        "w_proj_conv": nrm(ks[18], (L, CONV_CH, D), CONV_CH ** -0.5),
        "w_proj_attn": nrm(ks[19], (L, ATTN_V, D), ATTN_V ** -0.5),
        "w_proj_sgu": nrm(ks[20], (L, SGU_WIDTH, D), SGU_WIDTH ** -0.5),
        "w_out": nrm(ks[21], (L, D, D), D ** -0.5),
        "norm_ffn_pre": gain(ks[22], (L, D)),
        "norm_ffn_post": gain(ks[23], (L, D)),
        "w_ffn_up": nrm(ks[24], (L, D, D_FF), D ** -0.5),
        "w_ffn_down": nrm(ks[25], (L, D_FF, D), D_FF ** -0.5),
    }


def reference(x, norm_mix_pre, norm_mix_post, w_in, b_gate, conv_w, conv_b, conv_ln_g, conv_ln_b,
              lam_q1, lam_k1, lam_q2, lam_k2, subln_g, sgu_ln_g, sgu_ln_b, sgu_w, sgu_b,
              w_proj_conv, w_proj_attn, w_proj_sgu, w_out, norm_ffn_pre, norm_ffn_post,
              w_ffn_up, w_ffn_down):
    b, s, _ = x.shape
    slopes = alibi_slopes(ATTN_HEADS)
    for l in range(DEPTH):
        h = rms_norm(x, norm_mix_pre[l])
        z = h @ w_in[l]
        za, zb, q, k, v, su, sv, gl = jnp.split(z, SPLITS, axis=-1)

        a = za * jax.nn.sigmoid(zb)
        a = depthwise_conv(a, conv_w[l], conv_b[l])
        a = jax.nn.silu(layer_norm(a, conv_ln_g[l], conv_ln_b[l]))
        y_a = a @ w_proj_conv[l]

        lam_init = 0.8 - 0.6 * math.exp(-0.3 * l)
        lam = (jnp.exp(jnp.sum(lam_q1[l].astype(jnp.float32) * lam_k1[l].astype(jnp.float32)))
               - jnp.exp(jnp.sum(lam_q2[l].astype(jnp.float32) * lam_k2[l].astype(jnp.float32)))
               + lam_init)
        o = diff_attention(q.reshape(b, s, ATTN_HEADS, 2, ATTN_DH),
                           k.reshape(b, s, ATTN_HEADS, 2, ATTN_DH),
                           v.reshape(b, s, ATTN_HEADS, 2 * ATTN_DH), lam, slopes)
        o = rms_norm(o, subln_g[l]) * (1.0 - lam_init)
        y_b = o.reshape(b, s, ATTN_V) @ w_proj_attn[l]

        gu = jax.nn.gelu(su, approximate=False)
        gv = layer_norm(jax.nn.gelu(sv, approximate=False), sgu_ln_g[l], sgu_ln_b[l])
        y_c = spatial_gating(gu, gv, sgu_w[l], sgu_b[l]) @ w_proj_sgu[l]

        g_a, g_b, g_c = jnp.split(jax.nn.sigmoid(gl + b_gate[l]), N_BRANCH, axis=-1)
        mix = (g_a * y_a + g_b * y_b + g_c * y_c) @ w_out[l]
        x = x + rms_norm(mix, norm_mix_post[l])

        h = rms_norm(x, norm_ffn_pre[l])
        f = jnp.square(jax.nn.relu(h @ w_ffn_up[l])) @ w_ffn_down[l]
        x = x + rms_norm(f, norm_ffn_post[l])
    return x
```

```python
import math
from contextlib import ExitStack

import numpy as np
import concourse.bass as bass
import concourse.mybir as mybir
from concourse.bass_utils import run_bass_kernel_spmd

F32 = mybir.dt.float32
BF16 = mybir.dt.bfloat16
ALU = mybir.AluOpType
AF = mybir.ActivationFunctionType

FUSED = True

DEPTH = 2
T = 2048
D = 1024
TW = 512
NQ = 4
EPS = 1e-6
CONV_K = 31
NPP = 585
C_GPRE, C_GPOST, C_CONVB, C_CLNG, C_CLNB, C_BGATE, C_SUBG, C_FPRE, C_FPOST, C_CONVW, C_LAM = \
    0, 8, 16, 24, 32, 40, 64, 65, 73, 81, 329
C_SGUG, C_SGUB = 585, 593
NPP2 = 601

ENGS = ("pe", "act", "dve", "pool", "sp")
NROTS = {"pe": 16, "act": 4, "dve": 4, "pool": 2, "sp": 1}
LAST_READER_ONLY = True
NDMAK = {"sw": 48, "hw": 8}


class Res:
    __slots__ = ("name", "w", "rs")

    def __init__(self, name=""):
        self.name = name
        self.w = None
        self.rs = {}


class Op:
    __slots__ = ("eng", "fn", "deps", "signal", "sidx", "is_dma", "didx", "dkind")


class Prog:
    def __init__(self, nc):
        self.nc = nc
        self.streams = {e: [] for e in ENGS}
        self.dmas = {"sw": [], "hw": []}
        self.out_dmas = []
        self.pending = {e: [] for e in ENGS}
        self.dma_since_fence = []

    def fence(self):
        lasts = []
        for e in ENGS:
            for o in reversed(self.streams[e]):
                if not o.is_dma:
                    lasts.append(o)
                    break
        lasts += self.dma_since_fence
        self.dma_since_fence = []
        for e in ENGS:
            self.pending[e] = list(lasts)

    def op(self, eng, fn, reads=(), writes=(), dma=False, is_out=False):
        o = Op()
        o.eng = eng
        o.fn = fn
        o.signal = False
        o.sidx = 0
        o.is_dma = dma
        o.didx = -1
        o.dkind = "sw" if eng == "pool" else "hw"
        deps = {}
        for r in reads:
            if r.w is not None:
                deps[r.w] = True
        for w in writes:
            if w.w is not None and w.w not in deps:
                deps[w.w] = False
            for rd in w.rs.values():
                if rd not in deps:
                    deps[rd] = False
        final = []
        for d, raw in deps.items():
            if not dma and not d.is_dma and d.eng == eng:
                if eng == "pe":
                    continue
            final.append(d)
        if self.pending[eng]:
            for d in self.pending[eng]:
                if d not in deps and not (d.eng == eng and not d.is_dma and not dma):
                    final.append(d)
            self.pending[eng] = []
        if dma:
            lst = self.dmas[o.dkind]
            o.didx = len(lst)
            if o.didx >= NDMAK[o.dkind]:
                final.append(lst[o.didx - NDMAK[o.dkind]])
            lst.append(o)
            self.dma_since_fence.append(o)
            if is_out:
                self.out_dmas.append(o)
        o.deps = final
        for d in final:
            if not d.is_dma:
                d.signal = True
        rkey = ("dma", o.dkind, o.didx) if dma else (eng if LAST_READER_ONLY else id(o))
        for r in reads:
            r.rs[rkey] = o
        for w in writes:
            w.w = o
            w.rs = {}
        self.streams[eng].append(o)
        return o

    def emit(self, final_engine="sp"):
        nc = self.nc
        for e in ENGS:
            c = 0
            for o in self.streams[e]:
                if not o.is_dma and o.signal:
                    o.sidx = c
                    c += 1
        with ExitStack() as st:
            esem = {e: [st.enter_context(nc.semaphore(f"s_{e}{i}")) for i in range(NROTS[e])] for e in ENGS}
            dsem = {k: [st.enter_context(nc.semaphore(f"s_dma{k}{i}")) for i in range(n)] for k, n in NDMAK.items()}
            block = st.enter_context(nc.Block())
            out_dmas = self.out_dmas
            streams = self.streams

            def run(ename, eng):
                waited = {e: -1 for e in ENGS}
                dwaited = {}

                def dwait(d):
                    n = NDMAK[d.dkind]
                    s = (d.dkind, d.didx % n)
                    v = 16 * (d.didx // n + 1)
                    if dwaited.get(s, 0) < v:
                        eng.wait_ge(dsem[s[0]][s[1]], v)
                        dwaited[s] = v

                for o in streams[ename]:
                    for d in o.deps:
                        if d.is_dma:
                            dwait(d)
                        else:
                            if waited[d.eng] < d.sidx:
                                eng.wait_ge(esem[d.eng][d.sidx % NROTS[d.eng]], d.sidx // NROTS[d.eng] + 1)
                                waited[d.eng] = d.sidx
                    inst = o.fn(eng)
                    if o.is_dma:
                        inst.then_inc(dsem[o.dkind][o.didx % NDMAK[o.dkind]], 16)
                    elif o.signal:
                        inst.then_inc(esem[ename][o.sidx % NROTS[ename]], 1)
                if ename == final_engine:
                    for d in out_dmas:
                        dwait(d)

            @block.tensor
            def _(eng):
                run("pe", eng)

            @block.scalar
            def _(eng):
                run("act", eng)

            @block.vector
            def _(eng):
                run("dve", eng)

            @block.gpsimd
            def _(eng):
                run("pool", eng)

            @block.sync
            def _(eng):
                run("sp", eng)


class Pool:
    def __init__(self, tiles, res=None):
        self.tiles = tiles
        self.res = res if res is not None else [Res() for _ in tiles]
        self.i = 0

    def next(self):
        k = self.i % len(self.tiles)
        self.i += 1
        return self.tiles[k], self.res[k]


def build_nc(layers, dbg=False):
    nc = bass.Bass("TRN2", target_bir_lowering=False)
    nl = len(layers)
    xin = nc.dram_tensor("xT", [D, T], F32, kind="ExternalInput").ap()
    yout = nc.dram_tensor("yT", [D, T], F32, kind="ExternalOutput").ap()
    w_in = nc.dram_tensor("w_in", [nl, 80, 128, 8, 128], F32, kind="ExternalInput").ap()
    w_pa = nc.dram_tensor("w_pa", [nl, 8, 128, 8, 128], F32, kind="ExternalInput").ap()
    w_pb = nc.dram_tensor("w_pb", [nl, 8, 128, 8, 128], F32, kind="ExternalInput").ap()
    w_pc = nc.dram_tensor("w_pc", [nl, 8, 128, 8, 128], F32, kind="ExternalInput").ap()
    w_o = nc.dram_tensor("w_o", [nl, 8, 128, 8, 128], F32, kind="ExternalInput").ap()
    w_up = nc.dram_tensor("w_up", [nl, 32, 128, 8, 128], F32, kind="ExternalInput").ap()
    w_dn = nc.dram_tensor("w_dn", [nl, 8, 128, 32, 128], F32, kind="ExternalInput").ap()
    wsT = nc.dram_tensor("wsT", [nl, 128, 8, 128], F32, kind="ExternalInput").ap()
    sgub = nc.dram_tensor("sgub", [nl, 1, 1024], F32, kind="ExternalInput").ap()
    ppd = nc.dram_tensor("pp", [nl, 128, NPP2], F32, kind="ExternalInput").ap()
    identd = nc.dram_tensor("ident", [128, 128], F32, kind="ExternalInput").ap()
    sstripd = nc.dram_tensor("sstrip", [128, 896], F32, kind="ExternalInput").ap()
    lind = nc.dram_tensor("lin", [128, 512], F32, kind="ExternalInput").ap()
    btabd = nc.dram_tensor("btab", [128, 8 * 28], F32, kind="ExternalInput").ap()

    dbg_out = nc.dram_tensor("dbg", [D, T], F32, kind="ExternalOutput").ap() if dbg else None
    P = Prog(nc)
    op = P.op

    def MM(out, lhsT, rhs, start, stop):
        return lambda e: e.matmul(out, lhsT, rhs, start=start, stop=stop)

    def ACT(out, in_, func, bias=0.0, scale=1.0, accum_out=None):
        if accum_out is None:
            return lambda e: e.activation(out, in_, func, bias=bias, scale=scale)
        return lambda e: e.activation(out, in_, func, bias=bias, scale=scale, accum_out=accum_out)

    def TS(out, in0, s1, s2, op0, op1=None):
        if op1 is None:
            return lambda e: e.tensor_scalar(out, in0, s1, None, op0)
        return lambda e: e.tensor_scalar(out, in0, s1, s2, op0, op1)

    def STT(out, in0, scalar, in1, op0, op1):
        return lambda e: e.scalar_tensor_tensor(out, in0, scalar, in1, op0, op1)

    def TT(out, in0, in1, op_):
        return lambda e: e.tensor_tensor(out, in0, in1, op_)

    def CP(out, in_):
        return lambda e: e.tensor_copy(out, in_)

    def DMA(out, in_):
        return lambda e: e.dma_start(out=out, in_=in_)

    def RS(out, in_):
        return lambda e: e.reduce_sum(out, in_, mybir.AxisListType.X)

    def MS(ap, val):
        return lambda e: e.memset(ap, val)

    def RCP(out, in_):
        return lambda e: e.reciprocal(out, in_)

    with ExitStack() as st0:
        uniq = [0]

        def sb(st, name, shape, dt):
            uniq[0] += 1
            return st.enter_context(nc.sbuf_tensor(f"{name}_{uniq[0]}", shape, dt))

        xT = sb(st0, "xT_sb", [128, 8, T], F32)
        rx = [[Res() for _ in range(NQ)] for _ in range(8)]
        ones_f = sb(st0, "ones_f", [128, 128], F32)
        ones_b = sb(st0, "ones_b", [128, 128], BF16)
        ident_b = sb(st0, "ident_b", [128, 128], BF16)
        r_const = Res()
        pp = sb(st0, "pp_sb", [128, NPP2], F32)
        r_pp = Res()
        dp = sb(st0, "dp_sb", [128, 320], F32)
        r_dp = Res()
        banks = [st0.enter_context(nc.psum_tensor(f"ps{i}", [128, 512], F32)) for i in range(8)]
        bres = [Res() for _ in range(8)]
        ps_all = Pool(banks, bres)
        psA = Pool(banks[:4], bres[:4])
        psB = Pool(banks[4:], bres[4:])

        class _Cur:
            pass
        psums = _Cur()
        psums.pool = ps_all
        psums.next = lambda: psums.pool.next()
        tmps = Pool([sb(st0, f"tmp{i}", [128, 512], F32) for i in range(8)])
        L0 = sb(st0, "L0", [128, 512], F32); rL0 = Res()
        L1 = sb(st0, "L1", [128, 512], F32); rL1 = Res()
        L2 = sb(st0, "L2", [128, 512], F32); rL2 = Res()
        wpool = Pool([sb(st0, f"w{i}", [128, 8, 128], BF16) for i in range(6)])

        op("dve", MS(ones_f[:], 1.0), writes=[r_const])
        op("dve", MS(ones_b[:], 1.0), writes=[r_const])
        op("pool", DMA(ident_b[:], identd), writes=[r_const], dma=True)
        for c in range(8):
            op("sp", DMA(xT[:, c, :], xin[c * 128:(c + 1) * 128, :]), writes=rx[c], dma=True)

        def load_w(src):
            t, r = wpool.next()
            op("pool", DMA(t[:], src), writes=[r], dma=True)
            return t, r

        def rstd_bcast(ss_ps, r_ss, scale, eps, out_ap, r_out, n=TW):
            t1, r1 = tmps.next()
            op("act", ACT(t1[:, 0:n], ss_ps, AF.Ln, bias=eps_ap(eps), scale=scale), reads=[r_ss, r_const], writes=[r1])
            op("act", ACT(out_ap, t1[:, 0:n], AF.Exp, scale=-0.5), reads=[r1], writes=[r_out])

        epsc = sb(st0, "epsc", [128, 2], F32)
        op("dve", MS(epsc[:, 0:1], EPS), writes=[r_const])
        op("dve", MS(epsc[:, 1:2], 4.0 * EPS), writes=[r_const])

        def eps_ap(eps):
            return epsc[:, 0:1] if eps == EPS else epsc[:, 1:2]

        def sumsq_bcast(srcs, n=TW):
            ps, rps = psums.next()
            k = len(srcs)
            for i, (ap, rr) in enumerate(srcs):
                sq, rsq = tmps.next()
                op("act", ACT(sq[:, 0:n], ap, AF.Square), reads=rr, writes=[rsq])
                op("pe", MM(ps[:, 0:n], ones_f[:], sq[:, 0:n], i == 0, i == k - 1), reads=[rsq, r_const], writes=[rps])
            return ps, rps

        for li, l in enumerate(layers):
            lam_init = 0.8 - 0.6 * math.exp(-0.3 * l)
            op("sp", DMA(pp[:], ppd[li]), writes=[r_pp], dma=True)
            op("dve", TS(dp[:, 0:24], pp[:, C_BGATE:C_BGATE + 24], 0.5, None, ALU.mult), reads=[r_pp], writes=[r_dp])
            op("dve", TS(dp[:, 24:40], pp[:, C_CLNG:C_CLNG + 16], 0.5, None, ALU.mult), reads=[r_pp], writes=[r_dp])
            op("dve", TS(dp[:, 40:41], pp[:, C_SUBG:C_SUBG + 1], 1.0 - lam_init, None, ALU.mult), reads=[r_pp], writes=[r_dp])
            op("dve", TS(dp[:, 64:312], pp[:, C_CONVW:C_CONVW + 248], 0.5, None, ALU.mult), reads=[r_pp], writes=[r_dp])
            lt, rlt = tmps.next()
            op("dve", TT(lt[:, 0:64], pp[:, C_LAM:C_LAM + 64], pp[:, C_LAM + 64:C_LAM + 128], ALU.mult), reads=[r_pp], writes=[rlt])
            op("dve", TT(lt[:, 64:128], pp[:, C_LAM + 128:C_LAM + 192], pp[:, C_LAM + 192:C_LAM + 256], ALU.mult), reads=[r_pp, rlt], writes=[rlt])
            op("dve", RS(dp[:, 42:43], lt[:, 0:64]), reads=[rlt], writes=[r_dp])
            op("dve", RS(dp[:, 43:44], lt[:, 64:128]), reads=[rlt], writes=[r_dp])
            op("act", ACT(dp[:, 44:46], dp[:, 42:44], AF.Exp), reads=[r_dp], writes=[r_dp])
            op("dve", STT(dp[:, 41:42], dp[:, 45:46], -lam_init, dp[:, 44:45], ALU.add, ALU.subtract), reads=[r_dp], writes=[r_dp])

            with ExitStack() as st1:
                OT = sb(st1, "OT", [128, 8, T], BF16)
                rOT = [[Res() for _ in range(NQ)] for _ in range(8)]
                with ExitStack() as st:
                    hT = sb(st, "hT", [128, 8, T], BF16)
                    rh = [[Res() for _ in range(NQ)] for _ in range(8)]
                    QT = sb(st, "QT", [128, T], BF16); rQ = [Res() for _ in range(NQ)]
                    KT = sb(st, "KT", [128, T], BF16); rK = [Res() for _ in range(NQ)]
                    VH = sb(st, "VH", [128, 16, 128], BF16); rV = [Res() for _ in range(4)]
                    sstrip = sb(st, "sstrip_sb", [128, 896], F32)
                    lin = sb(st, "lin_sb", [128, 512], F32)
                    btab = sb(st, "btab_sb", [128, 8 * 28], F32)
                    r_tab = Res()
                    epool = Pool([sb(st, f"E{i}", [128, 512], BF16) for i in range(8)])
                    op("sp", DMA(sstrip[:], sstripd), writes=[r_tab], dma=True)
                    op("sp", DMA(lin[:], lind), writes=[r_tab], dma=True)
                    op("sp", DMA(btab[:], btabd), writes=[r_tab], dma=True)

                    psums.pool = psA
                    for tt in range(NQ):
                        sl = slice(tt * TW, (tt + 1) * TW)
                        ss, rss = sumsq_bcast([(xT[:, c, sl], [rx[c][tt]]) for c in range(8)])
                        rb, rrb = L0, rL0
                        rstd_bcast(ss[:], rss, 1.0 / D, EPS, rb[:], rrb)
                        for c in range(8):
                            op("dve", STT(hT[:, c, sl], xT[:, c, sl], pp[:, C_GPRE + c:C_GPRE + c + 1], rb[:], ALU.mult, ALU.mult),
                               reads=[rx[c][tt], rrb, r_pp], writes=[rh[c][tt]])

                    for h in range(8):
                        slope = 2.0 ** (-(h + 1))
                        wq, rwq = load_w(w_in[li, 16 + h])
                        wk, rwk = load_w(w_in[li, 24 + h])
                        wv, rwv = load_w(w_in[li, 32 + h])
                        for (w_, rw_, dst, rdst) in ((wq, rwq, QT, rQ), (wk, rwk, KT, rK)):
                            for tt in range(NQ):
                                sl = slice(tt * TW, (tt + 1) * TW)
                                ps, rps = psums.next()
                                for kc in range(8):
                                    op("pe", MM(ps[:], w_[:, kc, :], hT[:, kc, sl], kc == 0, kc == 7),
                                       reads=[rw_, rh[kc][tt]], writes=[rps])
                                op("act", ACT(dst[:, sl], ps[:], AF.Copy), reads=[rps], writes=[rdst[tt]])
                        for kg in range(4):
                            ps, rps = psums.next()
                            for kq in range(4):
                                kt = kg * 4 + kq
                                for kc in range(8):
                                    op("pe", MM(ps[:, kq * 128:(kq + 1) * 128], hT[:, kc, kt * 128:(kt + 1) * 128], wv[:, kc, :], kc == 0, kc == 7),
                                       reads=[rwv, rh[kc][kg]], writes=[rps])
                            op("dve", CP(VH[:, kg * 4:(kg + 1) * 4, :], ps[:].rearrange("p (k e) -> p k e", k=4)), reads=[rps], writes=[rV[kg]])
                        for qt in range(NQ):
                            qsl = slice(qt * TW, (qt + 1) * TW)
                            acc = [psB.next() for _ in range(4)]
                            def emit_S(kt, h=h, qt=qt, qsl=qsl, slope=slope):
                                ksl = slice(kt * 128, (kt + 1) * 128)
                                Dd = qt * TW - kt * 128
                                Es = []
                                for j in range(2):
                                    psl = slice(j * 64, (j + 1) * 64)
                                    ps, rps = psums.next()
                                    op("pe", MM(ps[:], KT[psl, ksl], QT[psl, qsl], True, True),
                                       reads=[rK[kt // 4], rQ[qt]], writes=[rps])
                                    tb, rtb = tmps.next()
                                    E, rE = epool.next()
                                    if -512 < Dd < 128:
                                        x0 = Dd + 384
                                        op("dve", STT(tb[:], sstrip[:, x0:x0 + 512], -8.0 * slope, ps[:], ALU.mult, ALU.add),
                                           reads=[rps, r_tab], writes=[rtb])
                                        op("act", ACT(E[:], tb[:], AF.Exp, scale=0.125), reads=[rtb], writes=[rE])
                                    else:
                                        sgn = -8.0 * slope if Dd >= 128 else 8.0 * slope
                                        di = (Dd + 1920) // 128
                                        op("dve", STT(tb[:], lin[:], sgn, ps[:], ALU.mult, ALU.add),
                                           reads=[rps, r_tab], writes=[rtb])
                                        op("act", ACT(E[:], tb[:], AF.Exp, bias=btab[:, h * 28 + di:h * 28 + di + 1], scale=0.125),
                                           reads=[rtb, r_tab], writes=[rE])
                                    Es.append((E, rE))
                                return Es

                            LA = 3
                            Eq = [emit_S(k) for k in range(LA)]
                            for kt in range(16):
                                Es = Eq.pop(0)
                                if kt + LA < 16:
                                    Eq.append(emit_S(kt + LA))
                                for j in range(2):
                                    E, rE = Es[j]
                                    op("pe", MM(acc[j][0][:], VH[:, kt, :], E[:], kt == 0, kt == 15),
                                       reads=[rV[kt // 4], rE], writes=[acc[j][1]])
                                    op("pe", MM(acc[2 + j][0][:], ones_b[:], E[:], kt == 0, kt == 15),
                                       reads=[r_const, rE], writes=[acc[2 + j][1]])
                            R0, rR0 = tmps.next(); R1, rR1 = tmps.next()
                            for (R_, rR_, zi) in ((R0, rR0, 2), (R1, rR1, 3)):
                                lz, rlz = tmps.next()
                                op("act", ACT(lz[:], acc[zi][0][:], AF.Ln), reads=[acc[zi][1]], writes=[rlz])
                                op("act", ACT(R_[:], lz[:], AF.Exp, scale=-1.0), reads=[rlz], writes=[rR_])
                            t0, rt0 = tmps.next(); t1, rt1 = tmps.next()
                            op("dve", TT(t0[:], acc[0][0][:], R0[:], ALU.mult), reads=[acc[0][1], rR0], writes=[rt0])
                            op("dve", TT(t1[:], acc[1][0][:], R1[:], ALU.mult), reads=[acc[1][1], rR1], writes=[rt1])
                            oo, roo = tmps.next()
                            op("dve", STT(oo[:], t1[:], dp[:, 41:42], t0[:], ALU.mult, ALU.add), reads=[rt0, rt1, r_dp], writes=[roo])
                            ss, rss = sumsq_bcast([(oo[:], [roo])])
                            rb, rrb = L0, rL0
                            rstd_bcast(ss[:], rss, 1.0 / 128.0, EPS, rb[:], rrb)
                            op("dve", STT(OT[:, h, qsl], oo[:], dp[:, 40:41], rb[:], ALU.mult, ALU.mult),
                               reads=[roo, rrb, r_dp], writes=[rOT[h][qt]])
                P.fence()
                psums.pool = ps_all

                with ExitStack() as st:
                    hq = sb(st, "hq", [128, 8, TW], BF16); rhq = Res()
                    hh = sb(st, "hh", [128, 8, 32], BF16); rhh = Res()
                    keep = sb(st, "keep", [128, 8, 16], BF16); rkeep = [Res() for _ in range(8)]
                    a2s = Pool([sb(st, f"a2_{i}", [128, TW + 30], BF16) for i in range(3)])
                    diag = sb(st, "diag", [128, CONV_K, 128], BF16); rdiag = Res()
                    FK = sb(st, "FK", [128, 8, TW], F32); rFK = [Res() for _ in range(8)]
                    A = sb(st, "A", [128, 8, TW], BF16); rA = [Res() for _ in range(8)]
                    Y = sb(st, "Y", [128, 8, TW], BF16); rY = [Res() for _ in range(8)]
                    MT = sb(st, "MT", [128, 8, TW], BF16); rMT = [Res() for _ in range(8)]
                    Bp = sb(st, "Bp", [128, 8, 128], F32); rBp = Res()
                    wsb = sb(st, "wsb", [128, 8, 128], BF16); rwsb = Res()
                    st4 = sb(st, "st4", [128, 24], F32); rst4 = Res()
                    wsf = FK[:, 0:2, :].rearrange("p c t -> p (c t)").rearrange("p (g t) -> p g t", g=8)
                    GELv = FK[:].rearrange("p c t -> p (c t)").rearrange("p (k f) -> p k f", k=4)
                    GVv = MT[:].rearrange("p c t -> p (c t)").rearrange("p (k f) -> p k f", k=4)

                    rwsf = rFK[0]
                    op("sp", DMA(wsf, wsT[li]), writes=[rFK[0], rFK[1]], dma=True)
                    op("dve", CP(wsb[:], wsf), reads=[rFK[0], rFK[1]], writes=[rwsb])
                    for hf in range(2):
                        ps, rps = psums.next()
                        op("pe", MM(ps[:], ones_f[:], FK[:, hf, :], True, True),
                           reads=[rFK[hf], r_const], writes=[rps])
                        sbr, rsbr = tmps.next()
                        op("sp", DMA(sbr[0:1, :], sgub[li][:, hf * 512:(hf + 1) * 512]), writes=[rsbr], dma=True)
                        ps2, rps2 = psums.next()
                        op("pe", MM(ps2[:], ones_f[0:1, :], sbr[0:1, :], True, True),
                           reads=[rsbr, r_const], writes=[rps2])
                        for g4 in range(4):
                            g = hf * 4 + g4
                            tb, rtb = tmps.next()
                            op("dve", CP(tb[:, 0:128], ps2[:, g4 * 128:(g4 + 1) * 128]), reads=[rps2], writes=[rtb])
                            op("dve", STT(Bp[:, g, :], ps[:, g4 * 128:(g4 + 1) * 128], pp[:, C_SGUB + g:C_SGUB + g + 1], tb[:, 0:128], ALU.mult, ALU.add),
                               reads=[rps, rtb, r_pp], writes=[rBp])

                    for tt in range(NQ):
                        lo = tt * TW
                        sl = slice(lo, lo + TW)
                        ss, rss = sumsq_bcast([(xT[:, c, sl], [rx[c][tt]]) for c in range(8)])
                        rb, rrb = L0, rL0
                        rstd_bcast(ss[:], rss, 1.0 / D, EPS, rb[:], rrb)
                        for c in range(8):
                            op("dve", STT(hq[:, c, :], xT[:, c, sl], pp[:, C_GPRE + c:C_GPRE + c + 1], rb[:], ALU.mult, ALU.mult),
                               reads=[rx[c][tt], rrb, r_pp], writes=[rhq])
                        has_l = tt > 0
                        has_r = tt < NQ - 1
                        if has_r:
                            tlo, tq, col = lo + TW, tt + 1, 15
                            hsl = slice(tlo, tlo + 15)
                            ssh, rssh = sumsq_bcast([(xT[:, c, hsl], [rx[c][tq]]) for c in range(8)], n=15)
                            rbh, rrbh = tmps.next()
                            rstd_bcast(ssh[:, 0:15], rssh, 1.0 / D, EPS, rbh[:, 0:15], rrbh, n=15)
                            for c in range(8):
                                op("dve", STT(hh[:, c, col:col + 15], xT[:, c, hsl], pp[:, C_GPRE + c:C_GPRE + c + 1], rbh[:, 0:15], ALU.mult, ALU.mult),
                                   reads=[rx[c][tq], rrbh, r_pp], writes=[rhh])

                        for g in range(8):
                            wsv, rwsv = load_w(w_in[li, 48 + g])
                            ps, rps = psums.next()
                            for kt in range(4):
                                for kc in range(8):
                                    op("pe", MM(ps[:, kt * 128:(kt + 1) * 128], hq[:, kc, kt * 128:(kt + 1) * 128], wsv[:, kc, :], kc == 0, kc == 7),
                                       reads=[rwsv, rhq], writes=[rps])
                            op("act", ACT(GELv[:, :, g * 128:(g + 1) * 128], ps[:].rearrange("p (k c) -> p k c", k=4), AF.Gelu),
                               reads=[rps], writes=rFK)
                        op("dve", MS(st4[:], 0.0), writes=[rst4])
                        op("dve", RS(st4[:, 0:4], GELv), reads=rFK, writes=[rst4])
                        for kt in range(4):
                            for hf in range(2):
                                jk, rjk = tmps.next()
                                op("act", ACT(jk[:], GELv[:, kt, hf * 512:(hf + 1) * 512], AF.Square, accum_out=st4[:, 16 + kt * 2 + hf:17 + kt * 2 + hf]),
                                   reads=rFK + [rst4], writes=[rjk, rst4])
                        op("dve", RS(st4[:, 4:8], st4[:, 16:24].rearrange("p (k h) -> p k h", h=2)), reads=[rst4], writes=[rst4])
                        op("dve", TS(st4[:, 8:12], st4[:, 0:4], 1.0 / D, None, ALU.mult), reads=[rst4], writes=[rst4])
                        op("dve", TT(st4[:, 12:16], st4[:, 8:12], st4[:, 8:12], ALU.mult), reads=[rst4], writes=[rst4])
                        op("dve", STT(st4[:, 4:8], st4[:, 4:8], 1.0 / D, st4[:, 12:16], ALU.mult, ALU.subtract), reads=[rst4], writes=[rst4])
                        op("act", ACT(st4[:, 12:16], st4[:, 4:8], AF.Ln, bias=epsc[:, 0:1]), reads=[rst4, r_const], writes=[rst4])
                        op("act", ACT(st4[:, 4:8], st4[:, 12:16], AF.Exp, scale=-0.5), reads=[rst4], writes=[rst4])
                        for kt in range(4):
                            op("dve", TS(GVv[:, kt, :], GELv[:, kt, :], st4[:, 8 + kt:9 + kt], st4[:, 4 + kt:5 + kt], ALU.subtract, ALU.mult),
                               reads=rFK + [rst4], writes=rMT)
                        def emit_glu(c, tt=tt, has_l=has_l, has_r=has_r):
                            wa, rwa = load_w(w_in[li, c])
                            wb, rwb = load_w(w_in[li, 8 + c])
                            psa, rpsa = psums.next()
                            psb, rpsb = psums.next()
                            for kc in range(8):
                                op("pe", MM(psa[:], wa[:, kc, :], hq[:, kc, :], kc == 0, kc == 7), reads=[rwa, rhq], writes=[rpsa])
                            for kc in range(8):
                                op("pe", MM(psb[:], wb[:, kc, :], hq[:, kc, :], kc == 0, kc == 7), reads=[rwb, rhq], writes=[rpsb])
                            a2, ra2 = a2s.next()
                            th, rth = tmps.next()
                            op("act", ACT(th[:], psb[:], AF.Tanh, scale=0.5), reads=[rpsb], writes=[rth])
                            op("dve", STT(a2[:, 15:15 + TW], th[:], 1.0, psa[:], ALU.add, ALU.mult), reads=[rth, rpsa], writes=[ra2])
                            if has_r:
                                psh, rpsh = psums.next()
                                for (w_, off) in ((wa, 0), (wb, 30)):
                                    for kc in range(8):
                                        op("pe", MM(psh[:, off + 15:off + 30], w_[:, kc, :], hh[:, kc, 15:30], kc == 0, kc == 7),
                                           reads=[rwa, rwb, rhh], writes=[rpsh])
                                thh, rthh = tmps.next()
                                op("act", ACT(thh[:, 15:30], psh[:, 45:60], AF.Tanh, scale=0.5), reads=[rpsh], writes=[rthh])
                                op("dve", STT(a2[:, 15 + TW:30 + TW], thh[:, 15:30], 1.0, psh[:, 15:30], ALU.add, ALU.mult), reads=[rthh, rpsh], writes=[ra2])
                            else:
                                op("dve", MS(a2[:, 15 + TW:30 + TW], 0.0), writes=[ra2])
                            if has_l:
                                op("dve", CP(a2[:, 0:15], keep[:, c, 0:15]), reads=[rkeep[c]], writes=[ra2])
                            else:
                                op("dve", MS(a2[:, 0:15], 0.0), writes=[ra2])
                            if has_r:
                                op("dve", CP(keep[:, c, 0:15], a2[:, TW:TW + 15]), reads=[ra2], writes=[rkeep[c]])
                            return a2, ra2

                        def emit_diag(c):
                            ident_bc = bass.AP(ident_b, 0, [[128, 128], [0, CONV_K], [1, 128]])
                            w_bc = bass.AP(dp, 64 + c * CONV_K, [[320, 128], [1, CONV_K], [0, 128]])
                            op("dve", TT(diag[:], ident_bc, w_bc, ALU.mult), reads=[r_const, r_dp], writes=[rdiag])

                        def emit_conv(c, a2, ra2):
                            psu, rpsu = psums.next()
                            for k in range(CONV_K):
                                op("pe", MM(psu[:], diag[:, k, :], a2[:, k:k + TW], k == 0, k == CONV_K - 1), reads=[rdiag, ra2], writes=[rpsu])
                            op("act", ACT(FK[:, c, :], psu[:], AF.Identity, bias=pp[:, C_CONVB + c:C_CONVB + c + 1]), reads=[rpsu, r_pp], writes=[rFK[c]])

                        nxt = emit_glu(0)
                        emit_diag(0)
                        for c in range(8):
                            cur = nxt
                            if c + 1 < 8:
                                nxt = emit_glu(c + 1)
                            emit_conv(c, *cur)
                            if c + 1 < 8:
                                emit_diag(c + 1)
                        ps1, rps1 = psums.next()
                        for c in range(8):
                            op("pe", MM(ps1[:], ones_f[:], FK[:, c, :], c == 0, c == 7), reads=[rFK[c], r_const], writes=[rps1])
                        ps2, rps2 = sumsq_bcast([(FK[:, c, :], [rFK[c]]) for c in range(8)])
                        mean, rmean = L1, rL1
                        op("dve", TS(mean[:], ps1[:], 1.0 / D, None, ALU.mult), reads=[rps1], writes=[rmean])
                        msq, rmsq = tmps.next()
                        op("dve", TT(msq[:], mean[:], mean[:], ALU.mult), reads=[rmean], writes=[rmsq])
                        var, rvar = tmps.next()
                        op("dve", STT(var[:], ps2[:], 1.0 / D, msq[:], ALU.mult, ALU.subtract), reads=[rps2, rmsq], writes=[rvar])
                        rs, rrs = L2, rL2
                        t1_, r1_ = tmps.next()
                        op("act", ACT(t1_[:], var[:], AF.Ln, bias=epsc[:, 0:1]), reads=[rvar, r_const], writes=[r1_])
                        op("act", ACT(rs[:], t1_[:], AF.Exp, scale=-0.5), reads=[r1_], writes=[rrs])
                        for c in range(8):
                            ta, rta = tmps.next()
                            op("dve", TT(ta[:], FK[:, c, :], mean[:], ALU.subtract), reads=[rFK[c], rmean], writes=[rta])
                            tb, rtb = tmps.next()
                            op("dve", TT(tb[:], ta[:], rs[:], ALU.mult), reads=[rta, rrs], writes=[rtb])
                            yh, ryh = tmps.next()
                            op("dve", TS(yh[:], tb[:], dp[:, 24 + c:25 + c], dp[:, 32 + c:33 + c], ALU.mult, ALU.add), reads=[rtb, r_dp], writes=[ryh])
                            th, rth = tmps.next()
                            op("act", ACT(th[:], yh[:], AF.Tanh), reads=[ryh], writes=[rth])
                            op("dve", STT(A[:, c, :], th[:], 1.0, yh[:], ALU.add, ALU.mult), reads=[rth, ryh], writes=[rA[c]])

                        for g in range(8):
                            wsu, rwsu = load_w(w_in[li, 40 + g])
                            psu_, rpsu_ = psums.next()
                            for kc in range(8):
                                op("pe", MM(psu_[:], wsu[:, kc, :], hq[:, kc, :], kc == 0, kc == 7), reads=[rwsu, rhq], writes=[rpsu_])
                            gu, rgu = tmps.next()
                            op("act", ACT(gu[:], psu_[:], AF.Gelu), reads=[rpsu_], writes=[rgu])
                            psm, rpsm = psums.next()
                            for n in range(4):
                                op("pe", MM(psm[:, n * 128:(n + 1) * 128], GVv[:, n, g * 128:(g + 1) * 128], wsb[:, g, :], True, True),
                                   reads=rMT + [rwsb], writes=[rpsm])
                            mx, rmx = tmps.next()
                            bp_bc = bass.AP(Bp, g * 128, [[1024, 128], [0, 4], [1, 128]])
                            op("dve", STT(mx[:].rearrange("p (n t) -> p n t", n=4), psm[:].rearrange("p (n t) -> p n t", n=4),
                                          pp[:, C_SGUG + g:C_SGUG + g + 1], bp_bc, ALU.mult, ALU.add),
                               reads=[rpsm, rBp, r_pp], writes=[rmx])
                            op("dve", TT(Y[:, g, :], gu[:], mx[:], ALU.mult), reads=[rgu, rmx], writes=[rY[g]])

                        for j in range(8):
                            terms = []
                            for (wp_d, src, rsrc, gch, hbcol) in ((w_pa, A, rA, 56 + j, j), (w_pb, None, None, 64 + j, 8 + j), (w_pc, Y, rY, 72 + j, 16 + j)):
                                wp, rwp = load_w(wp_d[li, j])
                                wg, rwg = load_w(w_in[li, gch])
                                pp_, rpp_ = psums.next()
                                pg_, rpg_ = psums.next()
                                for c in range(8):
                                    if src is None:
                                        op("pe", MM(pp_[:], wp[:, c, :], OT[:, c, sl], c == 0, c == 7), reads=[rwp, rOT[c][tt]], writes=[rpp_])
                                    else:
                                        op("pe", MM(pp_[:], wp[:, c, :], src[:, c, :], c == 0, c == 7), reads=[rwp, rsrc[c]], writes=[rpp_])
                                for kc in range(8):
                                    op("pe", MM(pg_[:], wg[:, kc, :], hq[:, kc, :], kc == 0, kc == 7), reads=[rwg, rhq], writes=[rpg_])
                                th, rth = tmps.next()
                                op("act", ACT(th[:], pg_[:], AF.Tanh, bias=dp[:, hbcol:hbcol + 1], scale=0.5), reads=[rpg_, r_dp], writes=[rth])
                                m_, rm_ = tmps.next()
                                op("dve", STT(m_[:], th[:], 1.0, pp_[:], ALU.add, ALU.mult), reads=[rth, rpp_], writes=[rm_])
                                terms.append((m_, rm_))
                            s_, rs_ = tmps.next()
                            op("dve", TT(s_[:], terms[0][0][:], terms[1][0][:], ALU.add), reads=[terms[0][1], terms[1][1]], writes=[rs_])
                            op("dve", TT(MT[:, j, :], s_[:], terms[2][0][:], ALU.add), reads=[rs_, terms[2][1]], writes=[rMT[j]])

                        for i in range(8):
                            wo, rwo = load_w(w_o[li, i])
                            ps, rps = psums.next()
                            for j in range(8):
                                op("pe", MM(ps[:], wo[:, j, :], MT[:, j, :], j == 0, j == 7), reads=[rwo, rMT[j]], writes=[rps])
                            op("act", ACT(FK[:, i, :], ps[:], AF.Copy), reads=[rps], writes=[rFK[i]])
                        ss, rss = sumsq_bcast([(FK[:, i, :], [rFK[i]]) for i in range(8)])
                        rb, rrb = L0, rL0
                        rstd_bcast(ss[:], rss, 1.0 / D, 4.0 * EPS, rb[:], rrb)
                        for i in range(8):
                            ta, rta = tmps.next()
                            op("dve", STT(ta[:], FK[:, i, :], pp[:, C_GPOST + i:C_GPOST + i + 1], rb[:], ALU.mult, ALU.mult), reads=[rFK[i], rrb, r_pp], writes=[rta])
                            op("dve", TT(xT[:, i, sl], xT[:, i, sl], ta[:], ALU.add), reads=[rx[i][tt], rta], writes=[rx[i][tt]])
                P.fence()

            with ExitStack() as st:
                h2 = sb(st, "h2", [128, 8, TW], BF16); rh2 = Res()
                hid = sb(st, "hid", [128, 32, TW], BF16); rhid = [Res() for _ in range(32)]
                Fb = sb(st, "Fb", [128, 8, TW], F32); rFb = [Res() for _ in range(8)]
                wdp = Pool([sb(st, f"wd{i}", [128, 32, 128], BF16) for i in range(2)])
                for tt in range(NQ):
                    sl = slice(tt * TW, (tt + 1) * TW)
                    ss, rss = sumsq_bcast([(xT[:, c, sl], [rx[c][tt]]) for c in range(8)])
                    rb, rrb = L0, rL0
                    rstd_bcast(ss[:], rss, 1.0 / D, EPS, rb[:], rrb)
                    for c in range(8):
                        op("dve", STT(h2[:, c, :], xT[:, c, sl], pp[:, C_FPRE + c:C_FPRE + c + 1], rb[:], ALU.mult, ALU.mult),
                           reads=[rx[c][tt], rrb, r_pp], writes=[rh2])
                    for m in range(32):
                        wu, rwu = load_w(w_up[li, m])
                        ps, rps = psums.next()
                        for kc in range(8):
                            op("pe", MM(ps[:], wu[:, kc, :], h2[:, kc, :], kc == 0, kc == 7), reads=[rwu, rh2], writes=[rps])
                        rl, rrl = tmps.next()
                        op("act", ACT(rl[:], ps[:], AF.Relu), reads=[rps], writes=[rrl])
                        op("dve", TT(hid[:, m, :], rl[:], rl[:], ALU.mult), reads=[rrl], writes=[rhid[m]])
                    for i in range(8):
                        wd, rwd = wdp.next()
                        op("pool", DMA(wd[:], w_dn[li, i]), writes=[rwd], dma=True)
                        ps, rps = psums.next()
                        for m in range(32):
                            op("pe", MM(ps[:], wd[:, m, :], hid[:, m, :], m == 0, m == 31), reads=[rwd, rhid[m]], writes=[rps])
                        op("act", ACT(Fb[:, i, :], ps[:], AF.Copy), reads=[rps], writes=[rFb[i]])
                    ss, rss = sumsq_bcast([(Fb[:, i, :], [rFb[i]]) for i in range(8)])
                    rb, rrb = L0, rL0
                    rstd_bcast(ss[:], rss, 1.0 / D, EPS, rb[:], rrb)
                    for i in range(8):
                        ta, rta = tmps.next()
                        op("dve", STT(ta[:], Fb[:, i, :], pp[:, C_FPOST + i:C_FPOST + i + 1], rb[:], ALU.mult, ALU.mult), reads=[rFb[i], rrb, r_pp], writes=[rta])
                        op("dve", TT(xT[:, i, sl], xT[:, i, sl], ta[:], ALU.add), reads=[rx[i][tt], rta], writes=[rx[i][tt]])
            P.fence()
            if dbg and li == 0:
                for c in range(8):
                    op("sp", DMA(dbg_out[c * 128:(c + 1) * 128, :], xT[:, c, :]), reads=rx[c], dma=True, is_out=True)

        for c in range(8):
            op("sp", DMA(yout[c * 128:(c + 1) * 128, :], xT[:, c, :]), reads=rx[c], dma=True, is_out=True)
        P.emit()
    return nc


def _tile_w(w, kc):
    K, N = w.shape
    return np.ascontiguousarray(w.reshape(kc, 128, N // 128, 128).transpose(2, 1, 0, 3))


def _fm(v):
    return np.ascontiguousarray(v.reshape(-1, 128).T)


def _prep_layer_inputs(inp, layers):
    f = lambda a: np.asarray(a, dtype=np.float32)
    out = {}
    out["w_in"] = np.stack([_tile_w(f(inp["w_in"][l]), 8) for l in layers])
    out["w_pa"] = np.stack([_tile_w(f(inp["w_proj_conv"][l]), 8) for l in layers])
    out["w_pb"] = np.stack([_tile_w(f(inp["w_proj_attn"][l]), 8) for l in layers])
    out["w_pc"] = np.stack([_tile_w(f(inp["w_proj_sgu"][l]), 8) for l in layers])
    out["w_o"] = np.stack([_tile_w(f(inp["w_out"][l]), 8) for l in layers])
    out["w_up"] = np.stack([_tile_w(f(inp["w_ffn_up"][l]), 8) for l in layers])
    out["w_dn"] = np.stack([_tile_w(f(inp["w_ffn_down"][l]), 32) for l in layers])
    out["wsT"] = np.stack([np.ascontiguousarray(f(inp["sgu_w"][l]).transpose(2, 0, 1)) for l in layers])
    out["sgub"] = np.stack([f(inp["sgu_b"][l]).reshape(1, 1024) for l in layers])
    pps = []
    for l in layers:
        p = np.zeros((128, NPP2), np.float32)
        p[:, C_GPRE:C_GPRE + 8] = _fm(f(inp["norm_mix_pre"][l]))
        p[:, C_GPOST:C_GPOST + 8] = _fm(f(inp["norm_mix_post"][l]))
        p[:, C_CONVB:C_CONVB + 8] = _fm(f(inp["conv_b"][l]))
        p[:, C_CLNG:C_CLNG + 8] = _fm(f(inp["conv_ln_g"][l]))
        p[:, C_CLNB:C_CLNB + 8] = _fm(f(inp["conv_ln_b"][l]))
        p[:, C_BGATE:C_BGATE + 24] = _fm(f(inp["b_gate"][l]))
        p[:, C_SUBG] = f(inp["subln_g"][l])
        p[:, C_FPRE:C_FPRE + 8] = _fm(f(inp["norm_ffn_pre"][l]))
        p[:, C_FPOST:C_FPOST + 8] = _fm(f(inp["norm_ffn_post"][l]))
        cw = f(inp["conv_w"][l])
        p[:, C_CONVW:C_CONVW + 248] = cw.T.reshape(8, 128, 31).transpose(1, 0, 2).reshape(128, 248)
        lam = np.concatenate([f(inp[k][l]) for k in ("lam_q1", "lam_k1", "lam_q2", "lam_k2")])
        p[:, C_LAM:C_LAM + 256] = np.broadcast_to(lam[None, :], (128, 256))
        p[:, C_SGUG:C_SGUG + 8] = _fm(f(inp["sgu_ln_g"][l]))
        p[:, C_SGUB:C_SGUB + 8] = _fm(f(inp["sgu_ln_b"][l]))
        pps.append(p)
    out["pp"] = np.stack(pps)
    return out


def _const_tables():
    pidx = np.arange(128, dtype=np.float64)[:, None]
    sstrip = np.abs(np.arange(896, dtype=np.float64)[None, :] - pidx - 384.0).astype(np.float32)
    lin = np.broadcast_to(np.arange(512, dtype=np.float32)[None, :], (128, 512)).copy()
    btab = np.zeros((128, 8, 28), np.float32)
    for h in range(8):
        slope = 2.0 ** (-(h + 1))
        for di in range(28):
            Dd = di * 128 - 1920
            btab[:, h, di] = (-slope * np.abs(Dd - pidx[:, 0])).astype(np.float32)
    return {"ident": np.eye(128, dtype=np.float32), "sstrip": sstrip, "lin": lin,
            "btab": btab.reshape(128, 8 * 28)}


_NC_CACHE = {}


def _run(xT_list, inp, layers):
    key = tuple(layers)
    if key not in _NC_CACHE:
        _NC_CACHE[key] = build_nc(layers)
    nc = _NC_CACHE[key]
    shared = _prep_layer_inputs(inp, layers)
    shared.update(_const_tables())
    in_maps = []
    for b in range(8):
        m = dict(shared)
        m["xT"] = xT_list[b]
        in_maps.append(m)
    res = run_bass_kernel_spmd(nc, in_maps, core_ids=list(range(8)))
    return [np.asarray(r["yT"], dtype=np.float32) for r in res.results]


def kernel(**inputs):
    x = np.asarray(inputs["x"], dtype=np.float32)
    xT = [np.ascontiguousarray(x[b].T) for b in range(8)]
    if FUSED:
        yT = _run(xT, inputs, [0, 1])
    else:
        yT = _run(xT, inputs, [0])
        yT = _run([np.ascontiguousarray(a) for a in yT], inputs, [1])
    return np.stack([a.T for a in yT]).astype(np.float32)
```

```python
import math
from contextlib import ExitStack

import numpy as np
import concourse.bass as bass
import concourse.mybir as mybir
from concourse.bass_utils import run_bass_kernel_spmd

F32 = mybir.dt.float32
BF16 = mybir.dt.bfloat16
ALU = mybir.AluOpType
AF = mybir.ActivationFunctionType

FUSED = True

DEPTH = 2
T = 2048
D = 1024
TW = 512
NQ = 4
EPS = 1e-6
CONV_K = 31
NPP = 585
C_GPRE, C_GPOST, C_CONVB, C_CLNG, C_CLNB, C_BGATE, C_SUBG, C_FPRE, C_FPOST, C_CONVW, C_LAM = \
    0, 8, 16, 24, 32, 40, 64, 65, 73, 81, 329
C_SGUG, C_SGUB = 585, 593
NPP2 = 601

ENGS = ("pe", "act", "dve", "pool", "sp")
NROTS = {"pe": 16, "act": 4, "dve": 4, "pool": 2, "sp": 1}
LAST_READER_ONLY = True
NDMAK = {"sw": 48, "hw": 8}


class Res:
    __slots__ = ("name", "w", "rs")

    def __init__(self, name=""):
        self.name = name
        self.w = None
        self.rs = {}


class Op:
    __slots__ = ("eng", "fn", "deps", "signal", "sidx", "is_dma", "didx", "dkind")


class Prog:
    def __init__(self, nc):
        self.nc = nc
        self.streams = {e: [] for e in ENGS}
        self.dmas = {"sw": [], "hw": []}
        self.out_dmas = []
        self.pending = {e: [] for e in ENGS}
        self.dma_since_fence = []

    def fence(self):
        lasts = []
        for e in ENGS:
            for o in reversed(self.streams[e]):
                if not o.is_dma:
                    lasts.append(o)
                    break
        lasts += self.dma_since_fence
        self.dma_since_fence = []
        for e in ENGS:
            self.pending[e] = list(lasts)

    def op(self, eng, fn, reads=(), writes=(), dma=False, is_out=False):
        o = Op()
        o.eng = eng
        o.fn = fn
        o.signal = False
        o.sidx = 0
        o.is_dma = dma
        o.didx = -1
        o.dkind = "sw" if eng == "pool" else "hw"
        deps = {}
        for r in reads:
            if r.w is not None:
                deps[r.w] = True
        for w in writes:
            if w.w is not None and w.w not in deps:
                deps[w.w] = False
            for rd in w.rs.values():
                if rd not in deps:
                    deps[rd] = False
        final = []
        for d, raw in deps.items():
            if not dma and not d.is_dma and d.eng == eng:
                if eng == "pe":
                    continue
            final.append(d)
        if self.pending[eng]:
            for d in self.pending[eng]:
                if d not in deps and not (d.eng == eng and not d.is_dma and not dma):
                    final.append(d)
            self.pending[eng] = []
        if dma:
            lst = self.dmas[o.dkind]
            o.didx = len(lst)
            if o.didx >= NDMAK[o.dkind]:
                final.append(lst[o.didx - NDMAK[o.dkind]])
            lst.append(o)
            self.dma_since_fence.append(o)
            if is_out:
                self.out_dmas.append(o)
        o.deps = final
        for d in final:
            if not d.is_dma:
                d.signal = True
        rkey = ("dma", o.dkind, o.didx) if dma else (eng if LAST_READER_ONLY else id(o))
        for r in reads:
            r.rs[rkey] = o
        for w in writes:
            w.w = o
            w.rs = {}
        self.streams[eng].append(o)
        return o

    def emit(self, final_engine="sp"):
        nc = self.nc
        for e in ENGS:
            c = 0
            for o in self.streams[e]:
                if not o.is_dma and o.signal:
                    o.sidx = c
                    c += 1
        with ExitStack() as st:
            esem = {e: [st.enter_context(nc.semaphore(f"s_{e}{i}")) for i in range(NROTS[e])] for e in ENGS}
            dsem = {k: [st.enter_context(nc.semaphore(f"s_dma{k}{i}")) for i in range(n)] for k, n in NDMAK.items()}
            block = st.enter_context(nc.Block())
            out_dmas = self.out_dmas
            streams = self.streams

            def run(ename, eng):
                waited = {e: -1 for e in ENGS}
                dwaited = {}

                def dwait(d):
                    n = NDMAK[d.dkind]
                    s = (d.dkind, d.didx % n)
                    v = 16 * (d.didx // n + 1)
                    if dwaited.get(s, 0) < v:
                        eng.wait_ge(dsem[s[0]][s[1]], v)
                        dwaited[s] = v

                for o in streams[ename]:
                    for d in o.deps:
                        if d.is_dma:
                            dwait(d)
                        else:
                            if waited[d.eng] < d.sidx:
                                eng.wait_ge(esem[d.eng][d.sidx % NROTS[d.eng]], d.sidx // NROTS[d.eng] + 1)
                                waited[d.eng] = d.sidx
                    inst = o.fn(eng)
                    if o.is_dma:
                        inst.then_inc(dsem[o.dkind][o.didx % NDMAK[o.dkind]], 16)
                    elif o.signal:
                        inst.then_inc(esem[ename][o.sidx % NROTS[ename]], 1)
                if ename == final_engine:
                    for d in out_dmas:
                        dwait(d)

            @block.tensor
            def _(eng):
                run("pe", eng)

            @block.scalar
            def _(eng):
                run("act", eng)

            @block.vector
            def _(eng):
                run("dve", eng)

            @block.gpsimd
            def _(eng):
                run("pool", eng)

            @block.sync
            def _(eng):
                run("sp", eng)


class Pool:
    def __init__(self, tiles, res=None):
        self.tiles = tiles
        self.res = res if res is not None else [Res() for _ in tiles]
        self.i = 0

    def next(self):
        k = self.i % len(self.tiles)
        self.i += 1
        return self.tiles[k], self.res[k]


def build_nc(layers, dbg=False):
    nc = bass.Bass("TRN2", target_bir_lowering=False)
    nl = len(layers)
    xin = nc.dram_tensor("xT", [D, T], F32, kind="ExternalInput").ap()
    yout = nc.dram_tensor("yT", [D, T], F32, kind="ExternalOutput").ap()
    w_in = nc.dram_tensor("w_in", [nl, 80, 128, 8, 128], F32, kind="ExternalInput").ap()
    w_pa = nc.dram_tensor("w_pa", [nl, 8, 128, 8, 128], F32, kind="ExternalInput").ap()
    w_pb = nc.dram_tensor("w_pb", [nl, 8, 128, 8, 128], F32, kind="ExternalInput").ap()
    w_pc = nc.dram_tensor("w_pc", [nl, 8, 128, 8, 128], F32, kind="ExternalInput").ap()
    w_o = nc.dram_tensor("w_o", [nl, 8, 128, 8, 128], F32, kind="ExternalInput").ap()
    w_up = nc.dram_tensor("w_up", [nl, 32, 128, 8, 128], F32, kind="ExternalInput").ap()
    w_dn = nc.dram_tensor("w_dn", [nl, 8, 128, 32, 128], F32, kind="ExternalInput").ap()
    wsT = nc.dram_tensor("wsT", [nl, 128, 8, 128], F32, kind="ExternalInput").ap()
    sgub = nc.dram_tensor("sgub", [nl, 1, 1024], F32, kind="ExternalInput").ap()
    ppd = nc.dram_tensor("pp", [nl, 128, NPP2], F32, kind="ExternalInput").ap()
    identd = nc.dram_tensor("ident", [128, 128], F32, kind="ExternalInput").ap()
    sstripd = nc.dram_tensor("sstrip", [128, 896], F32, kind="ExternalInput").ap()
    lind = nc.dram_tensor("lin", [128, 512], F32, kind="ExternalInput").ap()
    btabd = nc.dram_tensor("btab", [128, 8 * 28], F32, kind="ExternalInput").ap()

    dbg_out = nc.dram_tensor("dbg", [D, T], F32, kind="ExternalOutput").ap() if dbg else None
    P = Prog(nc)
    op = P.op

    def MM(out, lhsT, rhs, start, stop):
        return lambda e: e.matmul(out, lhsT, rhs, start=start, stop=stop)

    def ACT(out, in_, func, bias=0.0, scale=1.0, accum_out=None):
        if accum_out is None:
            return lambda e: e.activation(out, in_, func, bias=bias, scale=scale)
        return lambda e: e.activation(out, in_, func, bias=bias, scale=scale, accum_out=accum_out)

    def TS(out, in0, s1, s2, op0, op1=None):
        if op1 is None:
            return lambda e: e.tensor_scalar(out, in0, s1, None, op0)
        return lambda e: e.tensor_scalar(out, in0, s1, s2, op0, op1)

    def STT(out, in0, scalar, in1, op0, op1):
        return lambda e: e.scalar_tensor_tensor(out, in0, scalar, in1, op0, op1)

    def TT(out, in0, in1, op_):
        return lambda e: e.tensor_tensor(out, in0, in1, op_)

    def CP(out, in_):
        return lambda e: e.tensor_copy(out, in_)

    def DMA(out, in_):
        return lambda e: e.dma_start(out=out, in_=in_)

    def RS(out, in_):
        return lambda e: e.reduce_sum(out, in_, mybir.AxisListType.X)

    def MS(ap, val):
        return lambda e: e.memset(ap, val)

    def RCP(out, in_):
        return lambda e: e.reciprocal(out, in_)

    with ExitStack() as st0:
        uniq = [0]

        def sb(st, name, shape, dt):
            uniq[0] += 1
            return st.enter_context(nc.sbuf_tensor(f"{name}_{uniq[0]}", shape, dt))

        xT = sb(st0, "xT_sb", [128, 8, T], F32)
        rx = [[Res() for _ in range(NQ)] for _ in range(8)]
        ones_f = sb(st0, "ones_f", [128, 128], F32)
        ones_b = sb(st0, "ones_b", [128, 128], BF16)
        ident_b = sb(st0, "ident_b", [128, 128], BF16)
        r_const = Res()
        pp = sb(st0, "pp_sb", [128, NPP2], F32)
        r_pp = Res()
        dp = sb(st0, "dp_sb", [128, 320], F32)
        r_dp = Res()
        banks = [st0.enter_context(nc.psum_tensor(f"ps{i}", [128, 512], F32)) for i in range(8)]
        bres = [Res() for _ in range(8)]
        ps_all = Pool(banks, bres)
        psA = Pool(banks[:4], bres[:4])
        psB = Pool(banks[4:], bres[4:])

        class _Cur:
            pass
        psums = _Cur()
        psums.pool = ps_all
        psums.next = lambda: psums.pool.next()
        tmps = Pool([sb(st0, f"tmp{i}", [128, 512], F32) for i in range(8)])
        L0 = sb(st0, "L0", [128, 512], F32); rL0 = Res()
        L1 = sb(st0, "L1", [128, 512], F32); rL1 = Res()
        L2 = sb(st0, "L2", [128, 512], F32); rL2 = Res()
        wpool = Pool([sb(st0, f"w{i}", [128, 8, 128], BF16) for i in range(6)])

        op("dve", MS(ones_f[:], 1.0), writes=[r_const])
        op("dve", MS(ones_b[:], 1.0), writes=[r_const])
        op("pool", DMA(ident_b[:], identd), writes=[r_const], dma=True)
        for c in range(8):
            op("sp", DMA(xT[:, c, :], xin[c * 128:(c + 1) * 128, :]), writes=rx[c], dma=True)

        def load_w(src):
            t, r = wpool.next()
            op("pool", DMA(t[:], src), writes=[r], dma=True)
            return t, r

        def rstd_bcast(ss_ps, r_ss, scale, eps, out_ap, r_out, n=TW):
            t1, r1 = tmps.next()
            op("act", ACT(t1[:, 0:n], ss_ps, AF.Ln, bias=eps_ap(eps), scale=scale), reads=[r_ss, r_const], writes=[r1])
            op("act", ACT(out_ap, t1[:, 0:n], AF.Exp, scale=-0.5), reads=[r1], writes=[r_out])

        epsc = sb(st0, "epsc", [128, 2], F32)
        op("dve", MS(epsc[:, 0:1], EPS), writes=[r_const])
        op("dve", MS(epsc[:, 1:2], 4.0 * EPS), writes=[r_const])

        def eps_ap(eps):
            return epsc[:, 0:1] if eps == EPS else epsc[:, 1:2]

        def sumsq_bcast(srcs, n=TW):
            ps, rps = psums.next()
            k = len(srcs)
            for i, (ap, rr) in enumerate(srcs):
                sq, rsq = tmps.next()
                op("act", ACT(sq[:, 0:n], ap, AF.Square), reads=rr, writes=[rsq])
                op("pe", MM(ps[:, 0:n], ones_f[:], sq[:, 0:n], i == 0, i == k - 1), reads=[rsq, r_const], writes=[rps])
            return ps, rps

        for li, l in enumerate(layers):
            lam_init = 0.8 - 0.6 * math.exp(-0.3 * l)
            op("sp", DMA(pp[:], ppd[li]), writes=[r_pp], dma=True)
            op("dve", TS(dp[:, 0:24], pp[:, C_BGATE:C_BGATE + 24], 0.5, None, ALU.mult), reads=[r_pp], writes=[r_dp])
            op("dve", TS(dp[:, 24:40], pp[:, C_CLNG:C_CLNG + 16], 0.5, None, ALU.mult), reads=[r_pp], writes=[r_dp])
            op("dve", TS(dp[:, 40:41], pp[:, C_SUBG:C_SUBG + 1], 1.0 - lam_init, None, ALU.mult), reads=[r_pp], writes=[r_dp])
            op("dve", TS(dp[:, 64:312], pp[:, C_CONVW:C_CONVW + 248], 0.5, None, ALU.mult), reads=[r_pp], writes=[r_dp])
            lt, rlt = tmps.next()
            op("dve", TT(lt[:, 0:64], pp[:, C_LAM:C_LAM + 64], pp[:, C_LAM + 64:C_LAM + 128], ALU.mult), reads=[r_pp], writes=[rlt])
            op("dve", TT(lt[:, 64:128], pp[:, C_LAM + 128:C_LAM + 192], pp[:, C_LAM + 192:C_LAM + 256], ALU.mult), reads=[r_pp, rlt], writes=[rlt])
            op("dve", RS(dp[:, 42:43], lt[:, 0:64]), reads=[rlt], writes=[r_dp])
            op("dve", RS(dp[:, 43:44], lt[:, 64:128]), reads=[rlt], writes=[r_dp])
            op("act", ACT(dp[:, 44:46], dp[:, 42:44], AF.Exp), reads=[r_dp], writes=[r_dp])
            op("dve", STT(dp[:, 41:42], dp[:, 45:46], -lam_init, dp[:, 44:45], ALU.add, ALU.subtract), reads=[r_dp], writes=[r_dp])

            with ExitStack() as st1:
                OT = sb(st1, "OT", [128, 8, T], BF16)
                rOT = [[Res() for _ in range(NQ)] for _ in range(8)]
                with ExitStack() as st:
                    hT = sb(st, "hT", [128, 8, T], BF16)
                    rh = [[Res() for _ in range(NQ)] for _ in range(8)]
                    QT = sb(st, "QT", [128, T], BF16); rQ = [Res() for _ in range(NQ)]
                    KT = sb(st, "KT", [128, T], BF16); rK = [Res() for _ in range(NQ)]
                    VH = sb(st, "VH", [128, 16, 128], BF16); rV = [Res() for _ in range(4)]
                    sstrip = sb(st, "sstrip_sb", [128, 896], F32)
                    lin = sb(st, "lin_sb", [128, 512], F32)
                    btab = sb(st, "btab_sb", [128, 8 * 28], F32)
                    r_tab = Res()
                    epool = Pool([sb(st, f"E{i}", [128, 512], BF16) for i in range(8)])
                    op("sp", DMA(sstrip[:], sstripd), writes=[r_tab], dma=True)
                    op("sp", DMA(lin[:], lind), writes=[r_tab], dma=True)
                    op("sp", DMA(btab[:], btabd), writes=[r_tab], dma=True)

                    psums.pool = psA
                    for tt in range(NQ):
                        sl = slice(tt * TW, (tt + 1) * TW)
                        ss, rss = sumsq_bcast([(xT[:, c, sl], [rx[c][tt]]) for c in range(8)])
                        rb, rrb = L0, rL0
                        rstd_bcast(ss[:], rss, 1.0 / D, EPS, rb[:], rrb)
                        for c in range(8):
                            op("dve", STT(hT[:, c, sl], xT[:, c, sl], pp[:, C_GPRE + c:C_GPRE + c + 1], rb[:], ALU.mult, ALU.mult),
                               reads=[rx[c][tt], rrb, r_pp], writes=[rh[c][tt]])

                    for h in range(8):
                        slope = 2.0 ** (-(h + 1))
                        wq, rwq = load_w(w_in[li, 16 + h])
                        wk, rwk = load_w(w_in[li, 24 + h])
                        wv, rwv = load_w(w_in[li, 32 + h])
                        for (w_, rw_, dst, rdst) in ((wq, rwq, QT, rQ), (wk, rwk, KT, rK)):
                            for tt in range(NQ):
                                sl = slice(tt * TW, (tt + 1) * TW)
                                ps, rps = psums.next()
                                for kc in range(8):
                                    op("pe", MM(ps[:], w_[:, kc, :], hT[:, kc, sl], kc == 0, kc == 7),
                                       reads=[rw_, rh[kc][tt]], writes=[rps])
                                op("act", ACT(dst[:, sl], ps[:], AF.Copy), reads=[rps], writes=[rdst[tt]])
                        for kg in range(4):
                            ps, rps = psums.next()
                            for kq in range(4):
                                kt = kg * 4 + kq
                                for kc in range(8):
                                    op("pe", MM(ps[:, kq * 128:(kq + 1) * 128], hT[:, kc, kt * 128:(kt + 1) * 128], wv[:, kc, :], kc == 0, kc == 7),
                                       reads=[rwv, rh[kc][kg]], writes=[rps])
                            op("dve", CP(VH[:, kg * 4:(kg + 1) * 4, :], ps[:].rearrange("p (k e) -> p k e", k=4)), reads=[rps], writes=[rV[kg]])
                        for qt in range(NQ):
                            qsl = slice(qt * TW, (qt + 1) * TW)
                            acc = [psB.next() for _ in range(4)]
                            def emit_S(kt, h=h, qt=qt, qsl=qsl, slope=slope):
                                ksl = slice(kt * 128, (kt + 1) * 128)
                                Dd = qt * TW - kt * 128
                                Es = []
                                for j in range(2):
                                    psl = slice(j * 64, (j + 1) * 64)
                                    ps, rps = psums.next()
                                    op("pe", MM(ps[:], KT[psl, ksl], QT[psl, qsl], True, True),
                                       reads=[rK[kt // 4], rQ[qt]], writes=[rps])
                                    tb, rtb = tmps.next()
                                    E, rE = epool.next()
                                    if -512 < Dd < 128:
                                        x0 = Dd + 384
                                        op("dve", STT(tb[:], sstrip[:, x0:x0 + 512], -8.0 * slope, ps[:], ALU.mult, ALU.add),
                                           reads=[rps, r_tab], writes=[rtb])
                                        op("act", ACT(E[:], tb[:], AF.Exp, scale=0.125), reads=[rtb], writes=[rE])
                                    else:
                                        sgn = -8.0 * slope if Dd >= 128 else 8.0 * slope
                                        di = (Dd + 1920) // 128
                                        op("dve", STT(tb[:], lin[:], sgn, ps[:], ALU.mult, ALU.add),
                                           reads=[rps, r_tab], writes=[rtb])
                                        op("act", ACT(E[:], tb[:], AF.Exp, bias=btab[:, h * 28 + di:h * 28 + di + 1], scale=0.125),
                                           reads=[rtb, r_tab], writes=[rE])
                                    Es.append((E, rE))
                                return Es

                            LA = 3
                            Eq = [emit_S(k) for k in range(LA)]
                            for kt in range(16):
                                Es = Eq.pop(0)
                                if kt + LA < 16:
                                    Eq.append(emit_S(kt + LA))
                                for j in range(2):
                                    E, rE = Es[j]
                                    op("pe", MM(acc[j][0][:], VH[:, kt, :], E[:], kt == 0, kt == 15),
                                       reads=[rV[kt // 4], rE], writes=[acc[j][1]])
                                    op("pe", MM(acc[2 + j][0][:], ones_b[:], E[:], kt == 0, kt == 15),
                                       reads=[r_const, rE], writes=[acc[2 + j][1]])
                            R0, rR0 = tmps.next(); R1, rR1 = tmps.next()
                            for (R_, rR_, zi) in ((R0, rR0, 2), (R1, rR1, 3)):
                                lz, rlz = tmps.next()
                                op("act", ACT(lz[:], acc[zi][0][:], AF.Ln), reads=[acc[zi][1]], writes=[rlz])
                                op("act", ACT(R_[:], lz[:], AF.Exp, scale=-1.0), reads=[rlz], writes=[rR_])
                            t0, rt0 = tmps.next(); t1, rt1 = tmps.next()
                            op("dve", TT(t0[:], acc[0][0][:], R0[:], ALU.mult), reads=[acc[0][1], rR0], writes=[rt0])
                            op("dve", TT(t1[:], acc[1][0][:], R1[:], ALU.mult), reads=[acc[1][1], rR1], writes=[rt1])
                            oo, roo = tmps.next()
                            op("dve", STT(oo[:], t1[:], dp[:, 41:42], t0[:], ALU.mult, ALU.add), reads=[rt0, rt1, r_dp], writes=[roo])
                            ss, rss = sumsq_bcast([(oo[:], [roo])])
                            rb, rrb = L0, rL0
                            rstd_bcast(ss[:], rss, 1.0 / 128.0, EPS, rb[:], rrb)
                            op("dve", STT(OT[:, h, qsl], oo[:], dp[:, 40:41], rb[:], ALU.mult, ALU.mult),
                               reads=[roo, rrb, r_dp], writes=[rOT[h][qt]])
                P.fence()
                psums.pool = ps_all

                with ExitStack() as st:
                    hq = sb(st, "hq", [128, 8, TW], BF16); rhq = Res()
                    hh = sb(st, "hh", [128, 8, 32], BF16); rhh = Res()
                    keep = sb(st, "keep", [128, 8, 16], BF16); rkeep = [Res() for _ in range(8)]
                    a2s = Pool([sb(st, f"a2_{i}", [128, TW + 30], BF16) for i in range(3)])
                    diag = sb(st, "diag", [128, CONV_K, 128], BF16); rdiag = Res()
                    FK = sb(st, "FK", [128, 8, TW], F32); rFK = [Res() for _ in range(8)]
                    A = sb(st, "A", [128, 8, TW], BF16); rA = [Res() for _ in range(8)]
                    Y = sb(st, "Y", [128, 8, TW], BF16); rY = [Res() for _ in range(8)]
                    MT = sb(st, "MT", [128, 8, TW], BF16); rMT = [Res() for _ in range(8)]
                    Bp = sb(st, "Bp", [128, 8, 128], F32); rBp = Res()
                    wsb = sb(st, "wsb", [128, 8, 128], BF16); rwsb = Res()
                    st4 = sb(st, "st4", [128, 24], F32); rst4 = Res()
                    wsf = FK[:, 0:2, :].rearrange("p c t -> p (c t)").rearrange("p (g t) -> p g t", g=8)
                    GELv = FK[:].rearrange("p c t -> p (c t)").rearrange("p (k f) -> p k f", k=4)
                    GVv = MT[:].rearrange("p c t -> p (c t)").rearrange("p (k f) -> p k f", k=4)

                    rwsf = rFK[0]
                    op("sp", DMA(wsf, wsT[li]), writes=[rFK[0], rFK[1]], dma=True)
                    op("dve", CP(wsb[:], wsf), reads=[rFK[0], rFK[1]], writes=[rwsb])
                    for hf in range(2):
                        ps, rps = psums.next()
                        op("pe", MM(ps[:], ones_f[:], FK[:, hf, :], True, True),
                           reads=[rFK[hf], r_const], writes=[rps])
                        sbr, rsbr = tmps.next()
                        op("sp", DMA(sbr[0:1, :], sgub[li][:, hf * 512:(hf + 1) * 512]), writes=[rsbr], dma=True)
                        ps2, rps2 = psums.next()
                        op("pe", MM(ps2[:], ones_f[0:1, :], sbr[0:1, :], True, True),
                           reads=[rsbr, r_const], writes=[rps2])
                        for g4 in range(4):
                            g = hf * 4 + g4
                            tb, rtb = tmps.next()
                            op("dve", CP(tb[:, 0:128], ps2[:, g4 * 128:(g4 + 1) * 128]), reads=[rps2], writes=[rtb])
                            op("dve", STT(Bp[:, g, :], ps[:, g4 * 128:(g4 + 1) * 128], pp[:, C_SGUB + g:C_SGUB + g + 1], tb[:, 0:128], ALU.mult, ALU.add),
                               reads=[rps, rtb, r_pp], writes=[rBp])

                    for tt in range(NQ):
                        lo = tt * TW
                        sl = slice(lo, lo + TW)
                        ss, rss = sumsq_bcast([(xT[:, c, sl], [rx[c][tt]]) for c in range(8)])
                        rb, rrb = L0, rL0
                        rstd_bcast(ss[:], rss, 1.0 / D, EPS, rb[:], rrb)
                        for c in range(8):
                            op("dve", STT(hq[:, c, :], xT[:, c, sl], pp[:, C_GPRE + c:C_GPRE + c + 1], rb[:], ALU.mult, ALU.mult),
                               reads=[rx[c][tt], rrb, r_pp], writes=[rhq])
                        has_l = tt > 0
                        has_r = tt < NQ - 1
                        if has_r:
                            tlo, tq, col = lo + TW, tt + 1, 15
                            hsl = slice(tlo, tlo + 15)
                            ssh, rssh = sumsq_bcast([(xT[:, c, hsl], [rx[c][tq]]) for c in range(8)], n=15)
                            rbh, rrbh = tmps.next()
                            rstd_bcast(ssh[:, 0:15], rssh, 1.0 / D, EPS, rbh[:, 0:15], rrbh, n=15)
                            for c in range(8):
                                op("dve", STT(hh[:, c, col:col + 15], xT[:, c, hsl], pp[:, C_GPRE + c:C_GPRE + c + 1], rbh[:, 0:15], ALU.mult, ALU.mult),
                                   reads=[rx[c][tq], rrbh, r_pp], writes=[rhh])

                        for g in range(8):
                            wsv, rwsv = load_w(w_in[li, 48 + g])
                            ps, rps = psums.next()
                            for kt in range(4):
                                for kc in range(8):
                                    op("pe", MM(ps[:, kt * 128:(kt + 1) * 128], hq[:, kc, kt * 128:(kt + 1) * 128], wsv[:, kc, :], kc == 0, kc == 7),
                                       reads=[rwsv, rhq], writes=[rps])
                            op("act", ACT(GELv[:, :, g * 128:(g + 1) * 128], ps[:].rearrange("p (k c) -> p k c", k=4), AF.Gelu),
                               reads=[rps], writes=rFK)
                        op("dve", MS(st4[:], 0.0), writes=[rst4])
                        op("dve", RS(st4[:, 0:4], GELv), reads=rFK, writes=[rst4])
                        for kt in range(4):
                            for hf in range(2):
                                jk, rjk = tmps.next()
                                op("act", ACT(jk[:], GELv[:, kt, hf * 512:(hf + 1) * 512], AF.Square, accum_out=st4[:, 16 + kt * 2 + hf:17 + kt * 2 + hf]),
                                   reads=rFK + [rst4], writes=[rjk, rst4])
                        op("dve", RS(st4[:, 4:8], st4[:, 16:24].rearrange("p (k h) -> p k h", h=2)), reads=[rst4], writes=[rst4])
                        op("dve", TS(st4[:, 8:12], st4[:, 0:4], 1.0 / D, None, ALU.mult), reads=[rst4], writes=[rst4])
                        op("dve", TT(st4[:, 12:16], st4[:, 8:12], st4[:, 8:12], ALU.mult), reads=[rst4], writes=[rst4])
                        op("dve", STT(st4[:, 4:8], st4[:, 4:8], 1.0 / D, st4[:, 12:16], ALU.mult, ALU.subtract), reads=[rst4], writes=[rst4])
                        op("act", ACT(st4[:, 12:16], st4[:, 4:8], AF.Ln, bias=epsc[:, 0:1]), reads=[rst4, r_const], writes=[rst4])
                        op("act", ACT(st4[:, 4:8], st4[:, 12:16], AF.Exp, scale=-0.5), reads=[rst4], writes=[rst4])
                        for kt in range(4):
                            op("dve", TS(GVv[:, kt, :], GELv[:, kt, :], st4[:, 8 + kt:9 + kt], st4[:, 4 + kt:5 + kt], ALU.subtract, ALU.mult),
                               reads=rFK + [rst4], writes=rMT)
                        def emit_glu(c, tt=tt, has_l=has_l, has_r=has_r):
                            wa, rwa = load_w(w_in[li, c])
                            wb, rwb = load_w(w_in[li, 8 + c])
                            psa, rpsa = psums.next()
                            psb, rpsb = psums.next()
                            for kc in range(8):
                                op("pe", MM(psa[:], wa[:, kc, :], hq[:, kc, :], kc == 0, kc == 7), reads=[rwa, rhq], writes=[rpsa])
                            for kc in range(8):
                                op("pe", MM(psb[:], wb[:, kc, :], hq[:, kc, :], kc == 0, kc == 7), reads=[rwb, rhq], writes=[rpsb])
                            a2, ra2 = a2s.next()
                            th, rth = tmps.next()
                            op("act", ACT(th[:], psb[:], AF.Tanh, scale=0.5), reads=[rpsb], writes=[rth])
                            op("dve", STT(a2[:, 15:15 + TW], th[:], 1.0, psa[:], ALU.add, ALU.mult), reads=[rth, rpsa], writes=[ra2])
                            if has_r:
                                psh, rpsh = psums.next()
                                for (w_, off) in ((wa, 0), (wb, 30)):
                                    for kc in range(8):
                                        op("pe", MM(psh[:, off + 15:off + 30], w_[:, kc, :], hh[:, kc, 15:30], kc == 0, kc == 7),
                                           reads=[rwa, rwb, rhh], writes=[rpsh])
                                thh, rthh = tmps.next()
                                op("act", ACT(thh[:, 15:30], psh[:, 45:60], AF.Tanh, scale=0.5), reads=[rpsh], writes=[rthh])
                                op("dve", STT(a2[:, 15 + TW:30 + TW], thh[:, 15:30], 1.0, psh[:, 15:30], ALU.add, ALU.mult), reads=[rthh, rpsh], writes=[ra2])
                            else:
                                op("dve", MS(a2[:, 15 + TW:30 + TW], 0.0), writes=[ra2])
                            if has_l:
                                op("dve", CP(a2[:, 0:15], keep[:, c, 0:15]), reads=[rkeep[c]], writes=[ra2])
                            else:
                                op("dve", MS(a2[:, 0:15], 0.0), writes=[ra2])
                            if has_r:
                                op("dve", CP(keep[:, c, 0:15], a2[:, TW:TW + 15]), reads=[ra2], writes=[rkeep[c]])
                            return a2, ra2

                        def emit_diag(c):
                            ident_bc = bass.AP(ident_b, 0, [[128, 128], [0, CONV_K], [1, 128]])
                            w_bc = bass.AP(dp, 64 + c * CONV_K, [[320, 128], [1, CONV_K], [0, 128]])
                            op("dve", TT(diag[:], ident_bc, w_bc, ALU.mult), reads=[r_const, r_dp], writes=[rdiag])

                        def emit_conv(c, a2, ra2):
                            psu, rpsu = psums.next()
                            for k in range(CONV_K):
                                op("pe", MM(psu[:], diag[:, k, :], a2[:, k:k + TW], k == 0, k == CONV_K - 1), reads=[rdiag, ra2], writes=[rpsu])
                            op("act", ACT(FK[:, c, :], psu[:], AF.Identity, bias=pp[:, C_CONVB + c:C_CONVB + c + 1]), reads=[rpsu, r_pp], writes=[rFK[c]])

                        nxt = emit_glu(0)
                        emit_diag(0)
                        for c in range(8):
                            cur = nxt
                            if c + 1 < 8:
                                nxt = emit_glu(c + 1)
                            emit_conv(c, *cur)
                            if c + 1 < 8:
                                emit_diag(c + 1)
                        ps1, rps1 = psums.next()
                        for c in range(8):
                            op("pe", MM(ps1[:], ones_f[:], FK[:, c, :], c == 0, c == 7), reads=[rFK[c], r_const], writes=[rps1])
                        ps2, rps2 = sumsq_bcast([(FK[:, c, :], [rFK[c]]) for c in range(8)])
                        mean, rmean = L1, rL1
                        op("dve", TS(mean[:], ps1[:], 1.0 / D, None, ALU.mult), reads=[rps1], writes=[rmean])
                        msq, rmsq = tmps.next()
                        op("dve", TT(msq[:], mean[:], mean[:], ALU.mult), reads=[rmean], writes=[rmsq])
                        var, rvar = tmps.next()
                        op("dve", STT(var[:], ps2[:], 1.0 / D, msq[:], ALU.mult, ALU.subtract), reads=[rps2, rmsq], writes=[rvar])
                        rs, rrs = L2, rL2
                        t1_, r1_ = tmps.next()
                        op("act", ACT(t1_[:], var[:], AF.Ln, bias=epsc[:, 0:1]), reads=[rvar, r_const], writes=[r1_])
                        op("act", ACT(rs[:], t1_[:], AF.Exp, scale=-0.5), reads=[r1_], writes=[rrs])
                        for c in range(8):
                            ta, rta = tmps.next()
                            op("dve", TT(ta[:], FK[:, c, :], mean[:], ALU.subtract), reads=[rFK[c], rmean], writes=[rta])
                            tb, rtb = tmps.next()
                            op("dve", TT(tb[:], ta[:], rs[:], ALU.mult), reads=[rta, rrs], writes=[rtb])
                            yh, ryh = tmps.next()
                            op("act", ACT(yh[:], tb[:], AF.Identity, bias=dp[:, 32 + c:33 + c], scale=dp[:, 24 + c:25 + c]), reads=[rtb, r_dp], writes=[ryh])
                            th, rth = tmps.next()
                            op("act", ACT(th[:], tb[:], AF.Tanh, bias=dp[:, 32 + c:33 + c], scale=dp[:, 24 + c:25 + c]), reads=[rtb, r_dp], writes=[rth])
                            op("dve", STT(A[:, c, :], th[:], 1.0, yh[:], ALU.add, ALU.mult), reads=[rth, ryh], writes=[rA[c]])

                        for g in range(8):
                            wsu, rwsu = load_w(w_in[li, 40 + g])
                            psu_, rpsu_ = psums.next()
                            for kc in range(8):
                                op("pe", MM(psu_[:], wsu[:, kc, :], hq[:, kc, :], kc == 0, kc == 7), reads=[rwsu, rhq], writes=[rpsu_])
                            gu, rgu = tmps.next()
                            op("act", ACT(gu[:], psu_[:], AF.Gelu), reads=[rpsu_], writes=[rgu])
                            psm, rpsm = psums.next()
                            for n in range(4):
                                op("pe", MM(psm[:, n * 128:(n + 1) * 128], GVv[:, n, g * 128:(g + 1) * 128], wsb[:, g, :], True, True),
                                   reads=rMT + [rwsb], writes=[rpsm])
                            mx, rmx = tmps.next()
                            bp_bc = bass.AP(Bp, g * 128, [[1024, 128], [0, 4], [1, 128]])
                            op("dve", STT(mx[:].rearrange("p (n t) -> p n t", n=4), psm[:].rearrange("p (n t) -> p n t", n=4),
                                          pp[:, C_SGUG + g:C_SGUG + g + 1], bp_bc, ALU.mult, ALU.add),
                               reads=[rpsm, rBp, r_pp], writes=[rmx])
                            op("dve", TT(Y[:, g, :], gu[:], mx[:], ALU.mult), reads=[rgu, rmx], writes=[rY[g]])

                        for j in range(8):
                            terms = []
                            for (wp_d, src, rsrc, gch, hbcol) in ((w_pa, A, rA, 56 + j, j), (w_pb, None, None, 64 + j, 8 + j), (w_pc, Y, rY, 72 + j, 16 + j)):
                                wp, rwp = load_w(wp_d[li, j])
                                wg, rwg = load_w(w_in[li, gch])
                                pp_, rpp_ = psums.next()
                                pg_, rpg_ = psums.next()
                                for c in range(8):
                                    if src is None:
                                        op("pe", MM(pp_[:], wp[:, c, :], OT[:, c, sl], c == 0, c == 7), reads=[rwp, rOT[c][tt]], writes=[rpp_])
                                    else:
                                        op("pe", MM(pp_[:], wp[:, c, :], src[:, c, :], c == 0, c == 7), reads=[rwp, rsrc[c]], writes=[rpp_])
                                for kc in range(8):
                                    op("pe", MM(pg_[:], wg[:, kc, :], hq[:, kc, :], kc == 0, kc == 7), reads=[rwg, rhq], writes=[rpg_])
                                th, rth = tmps.next()
                                op("act", ACT(th[:], pg_[:], AF.Tanh, bias=dp[:, hbcol:hbcol + 1], scale=0.5), reads=[rpg_, r_dp], writes=[rth])
                                m_, rm_ = tmps.next()
                                op("dve", STT(m_[:], th[:], 1.0, pp_[:], ALU.add, ALU.mult), reads=[rth, rpp_], writes=[rm_])
                                terms.append((m_, rm_))
                            s_, rs_ = tmps.next()
                            op("dve", TT(s_[:], terms[0][0][:], terms[1][0][:], ALU.add), reads=[terms[0][1], terms[1][1]], writes=[rs_])
                            op("dve", TT(MT[:, j, :], s_[:], terms[2][0][:], ALU.add), reads=[rs_, terms[2][1]], writes=[rMT[j]])

                        for i in range(8):
                            wo, rwo = load_w(w_o[li, i])
                            ps, rps = psums.next()
                            for j in range(8):
                                op("pe", MM(ps[:], wo[:, j, :], MT[:, j, :], j == 0, j == 7), reads=[rwo, rMT[j]], writes=[rps])
                            op("act", ACT(FK[:, i, :], ps[:], AF.Copy), reads=[rps], writes=[rFK[i]])
                        ss, rss = sumsq_bcast([(FK[:, i, :], [rFK[i]]) for i in range(8)])
                        rb, rrb = L0, rL0
                        rstd_bcast(ss[:], rss, 1.0 / D, 4.0 * EPS, rb[:], rrb)
                        for i in range(8):
                            ta, rta = tmps.next()
                            op("dve", STT(ta[:], FK[:, i, :], pp[:, C_GPOST + i:C_GPOST + i + 1], rb[:], ALU.mult, ALU.mult), reads=[rFK[i], rrb, r_pp], writes=[rta])
                            op("dve", TT(xT[:, i, sl], xT[:, i, sl], ta[:], ALU.add), reads=[rx[i][tt], rta], writes=[rx[i][tt]])
                P.fence()

            with ExitStack() as st:
                h2 = sb(st, "h2", [128, 8, TW], BF16); rh2 = Res()
                hid = sb(st, "hid", [128, 32, TW], BF16); rhid = [Res() for _ in range(32)]
                Fb = sb(st, "Fb", [128, 8, TW], F32); rFb = [Res() for _ in range(8)]
                wdp = Pool([sb(st, f"wd{i}", [128, 32, 128], BF16) for i in range(2)])
                for tt in range(NQ):
                    sl = slice(tt * TW, (tt + 1) * TW)
                    ss, rss = sumsq_bcast([(xT[:, c, sl], [rx[c][tt]]) for c in range(8)])
                    rb, rrb = L0, rL0
                    rstd_bcast(ss[:], rss, 1.0 / D, EPS, rb[:], rrb)
                    for c in range(8):
                        op("dve", STT(h2[:, c, :], xT[:, c, sl], pp[:, C_FPRE + c:C_FPRE + c + 1], rb[:], ALU.mult, ALU.mult),
                           reads=[rx[c][tt], rrb, r_pp], writes=[rh2])
                    for m in range(32):
                        wu, rwu = load_w(w_up[li, m])
                        ps, rps = psums.next()
                        for kc in range(8):
                            op("pe", MM(ps[:], wu[:, kc, :], h2[:, kc, :], kc == 0, kc == 7), reads=[rwu, rh2], writes=[rps])
                        rl, rrl = tmps.next()
                        op("act", ACT(rl[:], ps[:], AF.Relu), reads=[rps], writes=[rrl])
                        op("dve", TT(hid[:, m, :], rl[:], rl[:], ALU.mult), reads=[rrl], writes=[rhid[m]])
                    for i in range(8):
                        wd, rwd = wdp.next()
                        op("pool", DMA(wd[:], w_dn[li, i]), writes=[rwd], dma=True)
                        ps, rps = psums.next()
                        for m in range(32):
                            op("pe", MM(ps[:], wd[:, m, :], hid[:, m, :], m == 0, m == 31), reads=[rwd, rhid[m]], writes=[rps])
                        op("act", ACT(Fb[:, i, :], ps[:], AF.Copy), reads=[rps], writes=[rFb[i]])
                    ss, rss = sumsq_bcast([(Fb[:, i, :], [rFb[i]]) for i in range(8)])
                    rb, rrb = L0, rL0
                    rstd_bcast(ss[:], rss, 1.0 / D, EPS, rb[:], rrb)
                    for i in range(8):
                        ta, rta = tmps.next()
                        op("dve", STT(ta[:], Fb[:, i, :], pp[:, C_FPOST + i:C_FPOST + i + 1], rb[:], ALU.mult, ALU.mult), reads=[rFb[i], rrb, r_pp], writes=[rta])
                        op("dve", TT(xT[:, i, sl], xT[:, i, sl], ta[:], ALU.add), reads=[rx[i][tt], rta], writes=[rx[i][tt]])
            P.fence()
            if dbg and li == 0:
                for c in range(8):
                    op("sp", DMA(dbg_out[c * 128:(c + 1) * 128, :], xT[:, c, :]), reads=rx[c], dma=True, is_out=True)

        for c in range(8):
            op("sp", DMA(yout[c * 128:(c + 1) * 128, :], xT[:, c, :]), reads=rx[c], dma=True, is_out=True)
        P.emit()
    return nc


def _tile_w(w, kc):
    K, N = w.shape
    return np.ascontiguousarray(w.reshape(kc, 128, N // 128, 128).transpose(2, 1, 0, 3))


def _fm(v):
    return np.ascontiguousarray(v.reshape(-1, 128).T)


def _prep_layer_inputs(inp, layers):
    f = lambda a: np.asarray(a, dtype=np.float32)
    out = {}
    out["w_in"] = np.stack([_tile_w(f(inp["w_in"][l]), 8) for l in layers])
    out["w_pa"] = np.stack([_tile_w(f(inp["w_proj_conv"][l]), 8) for l in layers])
    out["w_pb"] = np.stack([_tile_w(f(inp["w_proj_attn"][l]), 8) for l in layers])
    out["w_pc"] = np.stack([_tile_w(f(inp["w_proj_sgu"][l]), 8) for l in layers])
    out["w_o"] = np.stack([_tile_w(f(inp["w_out"][l]), 8) for l in layers])
    out["w_up"] = np.stack([_tile_w(f(inp["w_ffn_up"][l]), 8) for l in layers])
    out["w_dn"] = np.stack([_tile_w(f(inp["w_ffn_down"][l]), 32) for l in layers])
    out["wsT"] = np.stack([np.ascontiguousarray(f(inp["sgu_w"][l]).transpose(2, 0, 1)) for l in layers])
    out["sgub"] = np.stack([f(inp["sgu_b"][l]).reshape(1, 1024) for l in layers])
    pps = []
    for l in layers:
        p = np.zeros((128, NPP2), np.float32)
        p[:, C_GPRE:C_GPRE + 8] = _fm(f(inp["norm_mix_pre"][l]))
        p[:, C_GPOST:C_GPOST + 8] = _fm(f(inp["norm_mix_post"][l]))
        p[:, C_CONVB:C_CONVB + 8] = _fm(f(inp["conv_b"][l]))
        p[:, C_CLNG:C_CLNG + 8] = _fm(f(inp["conv_ln_g"][l]))
        p[:, C_CLNB:C_CLNB + 8] = _fm(f(inp["conv_ln_b"][l]))
        p[:, C_BGATE:C_BGATE + 24] = _fm(f(inp["b_gate"][l]))
        p[:, C_SUBG] = f(inp["subln_g"][l])
        p[:, C_FPRE:C_FPRE + 8] = _fm(f(inp["norm_ffn_pre"][l]))
        p[:, C_FPOST:C_FPOST + 8] = _fm(f(inp["norm_ffn_post"][l]))
        cw = f(inp["conv_w"][l])
        p[:, C_CONVW:C_CONVW + 248] = cw.T.reshape(8, 128, 31).transpose(1, 0, 2).reshape(128, 248)
        lam = np.concatenate([f(inp[k][l]) for k in ("lam_q1", "lam_k1", "lam_q2", "lam_k2")])
        p[:, C_LAM:C_LAM + 256] = np.broadcast_to(lam[None, :], (128, 256))
        p[:, C_SGUG:C_SGUG + 8] = _fm(f(inp["sgu_ln_g"][l]))
        p[:, C_SGUB:C_SGUB + 8] = _fm(f(inp["sgu_ln_b"][l]))
        pps.append(p)
    out["pp"] = np.stack(pps)
    return out


def _const_tables():
    pidx = np.arange(128, dtype=np.float64)[:, None]
    sstrip = np.abs(np.arange(896, dtype=np.float64)[None, :] - pidx - 384.0).astype(np.float32)
    lin = np.broadcast_to(np.arange(512, dtype=np.float32)[None, :], (128, 512)).copy()
    btab = np.zeros((128, 8, 28), np.float32)
    for h in range(8):
        slope = 2.0 ** (-(h + 1))
        for di in range(28):
            Dd = di * 128 - 1920
            btab[:, h, di] = (-slope * np.abs(Dd - pidx[:, 0])).astype(np.float32)
    return {"ident": np.eye(128, dtype=np.float32), "sstrip": sstrip, "lin": lin,
            "btab": btab.reshape(128, 8 * 28)}


_NC_CACHE = {}


def _run(xT_list, inp, layers):
    key = tuple(layers)
    if key not in _NC_CACHE:
        _NC_CACHE[key] = build_nc(layers)
    nc = _NC_CACHE[key]
    shared = _prep_layer_inputs(inp, layers)
    shared.update(_const_tables())
    in_maps = []
    for b in range(8):
        m = dict(shared)
        m["xT"] = xT_list[b]
        in_maps.append(m)
    res = run_bass_kernel_spmd(nc, in_maps, core_ids=list(range(8)))
    return [np.asarray(r["yT"], dtype=np.float32) for r in res.results]


def kernel(**inputs):
    x = np.asarray(inputs["x"], dtype=np.float32)
    xT = [np.ascontiguousarray(x[b].T) for b in range(8)]
    if FUSED:
        yT = _run(xT, inputs, [0, 1])
    else:
        yT = _run(xT, inputs, [0])
        yT = _run([np.ascontiguousarray(a) for a in yT], inputs, [1])
    return np.stack([a.T for a in yT]).astype(np.float32)
```

```python
import math
from contextlib import ExitStack

import numpy as np
import concourse.bass as bass
import concourse.mybir as mybir
from concourse.bass_utils import run_bass_kernel_spmd

F32 = mybir.dt.float32
BF16 = mybir.dt.bfloat16
ALU = mybir.AluOpType
AF = mybir.ActivationFunctionType

FUSED = True

DEPTH = 2
T = 2048
D = 1024
TW = 512
NQ = 4
EPS = 1e-6
CONV_K = 31
NPP = 585
C_GPRE, C_GPOST, C_CONVB, C_CLNG, C_CLNB, C_BGATE, C_SUBG, C_FPRE, C_FPOST, C_CONVW, C_LAM = \
    0, 8, 16, 24, 32, 40, 64, 65, 73, 81, 329
C_SGUG, C_SGUB = 585, 593
NPP2 = 601

ENGS = ("pe", "act", "dve", "pool", "sp")
NROTS = {"pe": 16, "act": 4, "dve": 4, "pool": 2, "sp": 1}
LAST_READER_ONLY = True
NDMAK = {"sw": 48, "hw": 8}


class Res:
    __slots__ = ("name", "w", "rs")

    def __init__(self, name=""):
        self.name = name
        self.w = None
        self.rs = {}


class Op:
    __slots__ = ("eng", "fn", "deps", "signal", "sidx", "is_dma", "didx", "dkind")


class Prog:
    def __init__(self, nc):
        self.nc = nc
        self.streams = {e: [] for e in ENGS}
        self.dmas = {"sw": [], "hw": []}
        self.out_dmas = []
        self.pending = {e: [] for e in ENGS}
        self.dma_since_fence = []

    def fence(self):
        lasts = []
        for e in ENGS:
            for o in reversed(self.streams[e]):
                if not o.is_dma:
                    lasts.append(o)
                    break
        lasts += self.dma_since_fence
        self.dma_since_fence = []
        for e in ENGS:
            self.pending[e] = list(lasts)

    def op(self, eng, fn, reads=(), writes=(), dma=False, is_out=False):
        o = Op()
        o.eng = eng
        o.fn = fn
        o.signal = False
        o.sidx = 0
        o.is_dma = dma
        o.didx = -1
        o.dkind = "sw" if eng == "pool" else "hw"
        deps = {}
        for r in reads:
            if r.w is not None:
                deps[r.w] = True
        for w in writes:
            if w.w is not None and w.w not in deps:
                deps[w.w] = False
            for rd in w.rs.values():
                if rd not in deps:
                    deps[rd] = False
        final = []
        for d, raw in deps.items():
            if not dma and not d.is_dma and d.eng == eng:
                if eng == "pe":
                    continue
            final.append(d)
        if self.pending[eng]:
            for d in self.pending[eng]:
                if d not in deps and not (d.eng == eng and not d.is_dma and not dma):
                    final.append(d)
            self.pending[eng] = []
        if dma:
            lst = self.dmas[o.dkind]
            o.didx = len(lst)
            if o.didx >= NDMAK[o.dkind]:
                final.append(lst[o.didx - NDMAK[o.dkind]])
            lst.append(o)
            self.dma_since_fence.append(o)
            if is_out:
                self.out_dmas.append(o)
        o.deps = final
        for d in final:
            if not d.is_dma:
                d.signal = True
        rkey = ("dma", o.dkind, o.didx) if dma else (eng if LAST_READER_ONLY else id(o))
        for r in reads:
            r.rs[rkey] = o
        for w in writes:
            w.w = o
            w.rs = {}
        self.streams[eng].append(o)
        return o

    def emit(self, final_engine="sp"):
        nc = self.nc
        for e in ENGS:
            c = 0
            for o in self.streams[e]:
                if not o.is_dma and o.signal:
                    o.sidx = c
                    c += 1
        with ExitStack() as st:
            esem = {e: [st.enter_context(nc.semaphore(f"s_{e}{i}")) for i in range(NROTS[e])] for e in ENGS}
            dsem = {k: [st.enter_context(nc.semaphore(f"s_dma{k}{i}")) for i in range(n)] for k, n in NDMAK.items()}
            block = st.enter_context(nc.Block())
            out_dmas = self.out_dmas
            streams = self.streams

            def run(ename, eng):
                waited = {e: -1 for e in ENGS}
                dwaited = {}

                def dwait(d):
                    n = NDMAK[d.dkind]
                    s = (d.dkind, d.didx % n)
                    v = 16 * (d.didx // n + 1)
                    if dwaited.get(s, 0) < v:
                        eng.wait_ge(dsem[s[0]][s[1]], v)
                        dwaited[s] = v

                for o in streams[ename]:
                    for d in o.deps:
                        if d.is_dma:
                            dwait(d)
                        else:
                            if waited[d.eng] < d.sidx:
                                eng.wait_ge(esem[d.eng][d.sidx % NROTS[d.eng]], d.sidx // NROTS[d.eng] + 1)
                                waited[d.eng] = d.sidx
                    inst = o.fn(eng)
                    if o.is_dma:
                        inst.then_inc(dsem[o.dkind][o.didx % NDMAK[o.dkind]], 16)
                    elif o.signal:
                        inst.then_inc(esem[ename][o.sidx % NROTS[ename]], 1)
                if ename == final_engine:
                    for d in out_dmas:
                        dwait(d)

            @block.tensor
            def _(eng):
                run("pe", eng)

            @block.scalar
            def _(eng):
                run("act", eng)

            @block.vector
            def _(eng):
                run("dve", eng)

            @block.gpsimd
            def _(eng):
                run("pool", eng)

            @block.sync
            def _(eng):
                run("sp", eng)


class Pool:
    def __init__(self, tiles, res=None):
        self.tiles = tiles
        self.res = res if res is not None else [Res() for _ in tiles]
        self.i = 0

    def next(self):
        k = self.i % len(self.tiles)
        self.i += 1
        return self.tiles[k], self.res[k]


def build_nc(layers, dbg=False):
    nc = bass.Bass("TRN2", target_bir_lowering=False)
    nl = len(layers)
    xin = nc.dram_tensor("xT", [D, T], F32, kind="ExternalInput").ap()
    yout = nc.dram_tensor("yT", [D, T], F32, kind="ExternalOutput").ap()
    w_in = nc.dram_tensor("w_in", [nl, 80, 128, 8, 128], F32, kind="ExternalInput").ap()
    w_pa = nc.dram_tensor("w_pa", [nl, 8, 128, 8, 128], F32, kind="ExternalInput").ap()
    w_pb = nc.dram_tensor("w_pb", [nl, 8, 128, 8, 128], F32, kind="ExternalInput").ap()
    w_pc = nc.dram_tensor("w_pc", [nl, 8, 128, 8, 128], F32, kind="ExternalInput").ap()
    w_o = nc.dram_tensor("w_o", [nl, 8, 128, 8, 128], F32, kind="ExternalInput").ap()
    w_up = nc.dram_tensor("w_up", [nl, 32, 128, 8, 128], F32, kind="ExternalInput").ap()
    w_dn = nc.dram_tensor("w_dn", [nl, 8, 128, 32, 128], F32, kind="ExternalInput").ap()
    wsT = nc.dram_tensor("wsT", [nl, 128, 8, 128], F32, kind="ExternalInput").ap()
    sgub = nc.dram_tensor("sgub", [nl, 1, 1024], F32, kind="ExternalInput").ap()
    ppd = nc.dram_tensor("pp", [nl, 128, NPP2], F32, kind="ExternalInput").ap()
    identd = nc.dram_tensor("ident", [128, 128], F32, kind="ExternalInput").ap()
    sstripd = nc.dram_tensor("sstrip", [128, 896], F32, kind="ExternalInput").ap()
    lind = nc.dram_tensor("lin", [128, 512], F32, kind="ExternalInput").ap()
    btabd = nc.dram_tensor("btab", [128, 8 * 28], F32, kind="ExternalInput").ap()

    dbg_out = nc.dram_tensor("dbg", [D, T], F32, kind="ExternalOutput").ap() if dbg else None
    P = Prog(nc)
    op = P.op

    def MM(out, lhsT, rhs, start, stop):
        return lambda e: e.matmul(out, lhsT, rhs, start=start, stop=stop)

    def ACT(out, in_, func, bias=0.0, scale=1.0, accum_out=None):
        if accum_out is None:
            return lambda e: e.activation(out, in_, func, bias=bias, scale=scale)
        return lambda e: e.activation(out, in_, func, bias=bias, scale=scale, accum_out=accum_out)

    def TS(out, in0, s1, s2, op0, op1=None):
        if op1 is None:
            return lambda e: e.tensor_scalar(out, in0, s1, None, op0)
        return lambda e: e.tensor_scalar(out, in0, s1, s2, op0, op1)

    def STT(out, in0, scalar, in1, op0, op1):
        return lambda e: e.scalar_tensor_tensor(out, in0, scalar, in1, op0, op1)

    def TT(out, in0, in1, op_):
        return lambda e: e.tensor_tensor(out, in0, in1, op_)

    def CP(out, in_):
        return lambda e: e.tensor_copy(out, in_)

    def DMA(out, in_):
        return lambda e: e.dma_start(out=out, in_=in_)

    def RS(out, in_):
        return lambda e: e.reduce_sum(out, in_, mybir.AxisListType.X)

    def MS(ap, val):
        return lambda e: e.memset(ap, val)

    def RCP(out, in_):
        return lambda e: e.reciprocal(out, in_)

    with ExitStack() as st0:
        uniq = [0]

        def sb(st, name, shape, dt):
            uniq[0] += 1
            return st.enter_context(nc.sbuf_tensor(f"{name}_{uniq[0]}", shape, dt))

        xT = sb(st0, "xT_sb", [128, 8, T], F32)
        rx = [[Res() for _ in range(NQ)] for _ in range(8)]
        ones_f = sb(st0, "ones_f", [128, 128], F32)
        ones_b = sb(st0, "ones_b", [128, 128], BF16)
        ident_b = sb(st0, "ident_b", [128, 128], BF16)
        r_const = Res()
        pp = sb(st0, "pp_sb", [128, NPP2], F32)
        r_pp = Res()
        dp = sb(st0, "dp_sb", [128, 320], F32)
        r_dp = Res()
        banks = [st0.enter_context(nc.psum_tensor(f"ps{i}", [128, 512], F32)) for i in range(8)]
        bres = [Res() for _ in range(8)]
        ps_all = Pool(banks, bres)
        psA = Pool(banks[:4], bres[:4])
        psB = Pool(banks[4:], bres[4:])

        class _Cur:
            pass
        psums = _Cur()
        psums.pool = ps_all
        psums.next = lambda: psums.pool.next()
        tmps = Pool([sb(st0, f"tmp{i}", [128, 512], F32) for i in range(8)])
        L0 = sb(st0, "L0", [128, 512], F32); rL0 = Res()
        L1 = sb(st0, "L1", [128, 512], F32); rL1 = Res()
        L2 = sb(st0, "L2", [128, 512], F32); rL2 = Res()
        wpool = Pool([sb(st0, f"w{i}", [128, 8, 128], BF16) for i in range(6)])

        op("dve", MS(ones_f[:], 1.0), writes=[r_const])
        op("dve", MS(ones_b[:], 1.0), writes=[r_const])
        op("pool", DMA(ident_b[:], identd), writes=[r_const], dma=True)
        for c in range(8):
            op("sp", DMA(xT[:, c, :], xin[c * 128:(c + 1) * 128, :]), writes=rx[c], dma=True)

        def load_w(src):
            t, r = wpool.next()
            op("pool", DMA(t[:], src), writes=[r], dma=True)
            return t, r

        def rstd_bcast(ss_ps, r_ss, scale, eps, out_ap, r_out, n=TW):
            t1, r1 = tmps.next()
            op("act", ACT(t1[:, 0:n], ss_ps, AF.Ln, bias=eps_ap(eps), scale=scale), reads=[r_ss, r_const], writes=[r1])
            op("act", ACT(out_ap, t1[:, 0:n], AF.Exp, scale=-0.5), reads=[r1], writes=[r_out])

        epsc = sb(st0, "epsc", [128, 2], F32)
        op("dve", MS(epsc[:, 0:1], EPS), writes=[r_const])
        op("dve", MS(epsc[:, 1:2], 4.0 * EPS), writes=[r_const])

        def eps_ap(eps):
            return epsc[:, 0:1] if eps == EPS else epsc[:, 1:2]

        def sumsq_bcast(srcs, n=TW):
            ps, rps = psums.next()
            k = len(srcs)
            for i, (ap, rr) in enumerate(srcs):
                sq, rsq = tmps.next()
                op("act", ACT(sq[:, 0:n], ap, AF.Square), reads=rr, writes=[rsq])
                op("pe", MM(ps[:, 0:n], ones_f[:], sq[:, 0:n], i == 0, i == k - 1), reads=[rsq, r_const], writes=[rps])
            return ps, rps

        for li, l in enumerate(layers):
            lam_init = 0.8 - 0.6 * math.exp(-0.3 * l)
            op("sp", DMA(pp[:], ppd[li]), writes=[r_pp], dma=True)
            op("dve", TS(dp[:, 0:24], pp[:, C_BGATE:C_BGATE + 24], 0.5, None, ALU.mult), reads=[r_pp], writes=[r_dp])
            op("dve", TS(dp[:, 24:40], pp[:, C_CLNG:C_CLNG + 16], 0.5, None, ALU.mult), reads=[r_pp], writes=[r_dp])
            op("dve", TS(dp[:, 40:41], pp[:, C_SUBG:C_SUBG + 1], 1.0 - lam_init, None, ALU.mult), reads=[r_pp], writes=[r_dp])
            op("dve", TS(dp[:, 64:312], pp[:, C_CONVW:C_CONVW + 248], 0.5, None, ALU.mult), reads=[r_pp], writes=[r_dp])
            lt, rlt = tmps.next()
            op("dve", TT(lt[:, 0:64], pp[:, C_LAM:C_LAM + 64], pp[:, C_LAM + 64:C_LAM + 128], ALU.mult), reads=[r_pp], writes=[rlt])
            op("dve", TT(lt[:, 64:128], pp[:, C_LAM + 128:C_LAM + 192], pp[:, C_LAM + 192:C_LAM + 256], ALU.mult), reads=[r_pp, rlt], writes=[rlt])
            op("dve", RS(dp[:, 42:43], lt[:, 0:64]), reads=[rlt], writes=[r_dp])
            op("dve", RS(dp[:, 43:44], lt[:, 64:128]), reads=[rlt], writes=[r_dp])
            op("act", ACT(dp[:, 44:46], dp[:, 42:44], AF.Exp), reads=[r_dp], writes=[r_dp])
            op("dve", STT(dp[:, 41:42], dp[:, 45:46], -lam_init, dp[:, 44:45], ALU.add, ALU.subtract), reads=[r_dp], writes=[r_dp])

            with ExitStack() as st1:
                OT = sb(st1, "OT", [128, 8, T], BF16)
                rOT = [[Res() for _ in range(NQ)] for _ in range(8)]
                with ExitStack() as st:
                    hT = sb(st, "hT", [128, 8, T], BF16)
                    rh = [[Res() for _ in range(NQ)] for _ in range(8)]
                    QT = sb(st, "QT", [128, T], BF16); rQ = [Res() for _ in range(NQ)]
                    KT = sb(st, "KT", [128, T], BF16); rK = [Res() for _ in range(NQ)]
                    VH = sb(st, "VH", [128, 16, 128], BF16); rV = [Res() for _ in range(4)]
                    sstrip = sb(st, "sstrip_sb", [128, 896], F32)
                    lin = sb(st, "lin_sb", [128, 512], F32)
                    btab = sb(st, "btab_sb", [128, 8 * 28], F32)
                    r_tab = Res()
                    epool = Pool([sb(st, f"E{i}", [128, 512], BF16) for i in range(8)])
                    op("sp", DMA(sstrip[:], sstripd), writes=[r_tab], dma=True)
                    op("sp", DMA(lin[:], lind), writes=[r_tab], dma=True)
                    op("sp", DMA(btab[:], btabd), writes=[r_tab], dma=True)

                    psums.pool = psA
                    for tt in range(NQ):
                        sl = slice(tt * TW, (tt + 1) * TW)
                        ss, rss = sumsq_bcast([(xT[:, c, sl], [rx[c][tt]]) for c in range(8)])
                        rb, rrb = L0, rL0
                        rstd_bcast(ss[:], rss, 1.0 / D, EPS, rb[:], rrb)
                        for c in range(8):
                            op("dve", STT(hT[:, c, sl], xT[:, c, sl], pp[:, C_GPRE + c:C_GPRE + c + 1], rb[:], ALU.mult, ALU.mult),
                               reads=[rx[c][tt], rrb, r_pp], writes=[rh[c][tt]])

                    for h in range(8):
                        slope = 2.0 ** (-(h + 1))
                        wq, rwq = load_w(w_in[li, 16 + h])
                        wk, rwk = load_w(w_in[li, 24 + h])
                        wv, rwv = load_w(w_in[li, 32 + h])
                        for (w_, rw_, dst, rdst) in ((wq, rwq, QT, rQ), (wk, rwk, KT, rK)):
                            for tt in range(NQ):
                                sl = slice(tt * TW, (tt + 1) * TW)
                                ps, rps = psums.next()
                                for kc in range(8):
                                    op("pe", MM(ps[:], w_[:, kc, :], hT[:, kc, sl], kc == 0, kc == 7),
                                       reads=[rw_, rh[kc][tt]], writes=[rps])
                                op("act", ACT(dst[:, sl], ps[:], AF.Copy), reads=[rps], writes=[rdst[tt]])
                        for kg in range(4):
                            ps, rps = psums.next()
                            for kq in range(4):
                                kt = kg * 4 + kq
                                for kc in range(8):
                                    op("pe", MM(ps[:, kq * 128:(kq + 1) * 128], hT[:, kc, kt * 128:(kt + 1) * 128], wv[:, kc, :], kc == 0, kc == 7),
                                       reads=[rwv, rh[kc][kg]], writes=[rps])
                            op("dve", CP(VH[:, kg * 4:(kg + 1) * 4, :], ps[:].rearrange("p (k e) -> p k e", k=4)), reads=[rps], writes=[rV[kg]])
                        for qt in range(NQ):
                            qsl = slice(qt * TW, (qt + 1) * TW)
                            acc = [psB.next() for _ in range(4)]
                            def emit_S(kt, h=h, qt=qt, qsl=qsl, slope=slope):
                                ksl = slice(kt * 128, (kt + 1) * 128)
                                Dd = qt * TW - kt * 128
                                Es = []
                                for j in range(2):
                                    psl = slice(j * 64, (j + 1) * 64)
                                    ps, rps = psums.next()
                                    op("pe", MM(ps[:], KT[psl, ksl], QT[psl, qsl], True, True),
                                       reads=[rK[kt // 4], rQ[qt]], writes=[rps])
                                    tb, rtb = tmps.next()
                                    E, rE = epool.next()
                                    if -512 < Dd < 128:
                                        x0 = Dd + 384
                                        op("dve", STT(tb[:], sstrip[:, x0:x0 + 512], -8.0 * slope, ps[:], ALU.mult, ALU.add),
                                           reads=[rps, r_tab], writes=[rtb])
                                        op("act", ACT(E[:], tb[:], AF.Exp, scale=0.125), reads=[rtb], writes=[rE])
                                    else:
                                        sgn = -8.0 * slope if Dd >= 128 else 8.0 * slope
                                        di = (Dd + 1920) // 128
                                        op("dve", STT(tb[:], lin[:], sgn, ps[:], ALU.mult, ALU.add),
                                           reads=[rps, r_tab], writes=[rtb])
                                        op("act", ACT(E[:], tb[:], AF.Exp, bias=btab[:, h * 28 + di:h * 28 + di + 1], scale=0.125),
                                           reads=[rtb, r_tab], writes=[rE])
                                    Es.append((E, rE))
                                return Es

                            LA = 3
                            Eq = [emit_S(k) for k in range(LA)]
                            for kt in range(16):
                                Es = Eq.pop(0)
                                if kt + LA < 16:
                                    Eq.append(emit_S(kt + LA))
                                for j in range(2):
                                    E, rE = Es[j]
                                    op("pe", MM(acc[j][0][:], VH[:, kt, :], E[:], kt == 0, kt == 15),
                                       reads=[rV[kt // 4], rE], writes=[acc[j][1]])
                                    op("pe", MM(acc[2 + j][0][:], ones_b[:], E[:], kt == 0, kt == 15),
                                       reads=[r_const, rE], writes=[acc[2 + j][1]])
                            R0, rR0 = tmps.next(); R1, rR1 = tmps.next()
                            for (R_, rR_, zi) in ((R0, rR0, 2), (R1, rR1, 3)):
                                lz, rlz = tmps.next()
                                op("act", ACT(lz[:], acc[zi][0][:], AF.Ln), reads=[acc[zi][1]], writes=[rlz])
                                op("act", ACT(R_[:], lz[:], AF.Exp, scale=-1.0), reads=[rlz], writes=[rR_])
                            t0, rt0 = tmps.next(); t1, rt1 = tmps.next()
                            op("dve", TT(t0[:], acc[0][0][:], R0[:], ALU.mult), reads=[acc[0][1], rR0], writes=[rt0])
                            op("dve", TT(t1[:], acc[1][0][:], R1[:], ALU.mult), reads=[acc[1][1], rR1], writes=[rt1])
                            oo, roo = tmps.next()
                            op("dve", STT(oo[:], t1[:], dp[:, 41:42], t0[:], ALU.mult, ALU.add), reads=[rt0, rt1, r_dp], writes=[roo])
                            ss, rss = sumsq_bcast([(oo[:], [roo])])
                            rb, rrb = L0, rL0
                            rstd_bcast(ss[:], rss, 1.0 / 128.0, EPS, rb[:], rrb)
                            op("dve", STT(OT[:, h, qsl], oo[:], dp[:, 40:41], rb[:], ALU.mult, ALU.mult),
                               reads=[roo, rrb, r_dp], writes=[rOT[h][qt]])
                P.fence()
                psums.pool = ps_all

                with ExitStack() as st:
                    hq = sb(st, "hq", [128, 8, TW], BF16); rhq = Res()
                    hh = sb(st, "hh", [128, 8, 32], BF16); rhh = Res()
                    keep = sb(st, "keep", [128, 8, 16], BF16); rkeep = [Res() for _ in range(8)]
                    a2s = Pool([sb(st, f"a2_{i}", [128, TW + 30], BF16) for i in range(3)])
                    diag = sb(st, "diag", [128, CONV_K, 128], BF16); rdiag = Res()
                    FK = sb(st, "FK", [128, 8, TW], F32); rFK = [Res() for _ in range(8)]
                    A = sb(st, "A", [128, 8, TW], BF16); rA = [Res() for _ in range(8)]
                    Y = sb(st, "Y", [128, 8, TW], BF16); rY = [Res() for _ in range(8)]
                    MT = sb(st, "MT", [128, 8, TW], BF16); rMT = [Res() for _ in range(8)]
                    Bp = sb(st, "Bp", [128, 8, 128], F32); rBp = Res()
                    wsb = sb(st, "wsb", [128, 8, 128], BF16); rwsb = Res()
                    st4 = sb(st, "st4", [128, 24], F32); rst4 = Res()
                    wsf = FK[:, 0:2, :].rearrange("p c t -> p (c t)").rearrange("p (g t) -> p g t", g=8)
                    GELv = FK[:].rearrange("p c t -> p (c t)").rearrange("p (k f) -> p k f", k=4)
                    GVv = MT[:].rearrange("p c t -> p (c t)").rearrange("p (k f) -> p k f", k=4)

                    rwsf = rFK[0]
                    op("sp", DMA(wsf, wsT[li]), writes=[rFK[0], rFK[1]], dma=True)
                    op("dve", CP(wsb[:], wsf), reads=[rFK[0], rFK[1]], writes=[rwsb])
                    for hf in range(2):
                        ps, rps = psums.next()
                        op("pe", MM(ps[:], ones_f[:], FK[:, hf, :], True, True),
                           reads=[rFK[hf], r_const], writes=[rps])
                        sbr, rsbr = tmps.next()
                        op("sp", DMA(sbr[0:1, :], sgub[li][:, hf * 512:(hf + 1) * 512]), writes=[rsbr], dma=True)
                        ps2, rps2 = psums.next()
                        op("pe", MM(ps2[:], ones_f[0:1, :], sbr[0:1, :], True, True),
                           reads=[rsbr, r_const], writes=[rps2])
                        for g4 in range(4):
                            g = hf * 4 + g4
                            tb, rtb = tmps.next()
                            op("dve", CP(tb[:, 0:128], ps2[:, g4 * 128:(g4 + 1) * 128]), reads=[rps2], writes=[rtb])
                            op("dve", STT(Bp[:, g, :], ps[:, g4 * 128:(g4 + 1) * 128], pp[:, C_SGUB + g:C_SGUB + g + 1], tb[:, 0:128], ALU.mult, ALU.add),
                               reads=[rps, rtb, r_pp], writes=[rBp])

                    for tt in range(NQ):
                        lo = tt * TW
                        sl = slice(lo, lo + TW)
                        ss, rss = sumsq_bcast([(xT[:, c, sl], [rx[c][tt]]) for c in range(8)])
                        rb, rrb = L0, rL0
                        rstd_bcast(ss[:], rss, 1.0 / D, EPS, rb[:], rrb)
                        for c in range(8):
                            op("dve", STT(hq[:, c, :], xT[:, c, sl], pp[:, C_GPRE + c:C_GPRE + c + 1], rb[:], ALU.mult, ALU.mult),
                               reads=[rx[c][tt], rrb, r_pp], writes=[rhq])
                        has_l = tt > 0
                        has_r = tt < NQ - 1
                        if has_r:
                            tlo, tq, col = lo + TW, tt + 1, 15
                            hsl = slice(tlo, tlo + 15)
                            ssh, rssh = sumsq_bcast([(xT[:, c, hsl], [rx[c][tq]]) for c in range(8)], n=15)
                            rbh, rrbh = tmps.next()
                            rstd_bcast(ssh[:, 0:15], rssh, 1.0 / D, EPS, rbh[:, 0:15], rrbh, n=15)
                            for c in range(8):
                                op("dve", STT(hh[:, c, col:col + 15], xT[:, c, hsl], pp[:, C_GPRE + c:C_GPRE + c + 1], rbh[:, 0:15], ALU.mult, ALU.mult),
                                   reads=[rx[c][tq], rrbh, r_pp], writes=[rhh])

                        def emit_glu(c, tt=tt, has_l=has_l, has_r=has_r):
                            wa, rwa = load_w(w_in[li, c])
                            wb, rwb = load_w(w_in[li, 8 + c])
                            psa, rpsa = psums.next()
                            psb, rpsb = psums.next()
                            for kc in range(8):
                                op("pe", MM(psa[:], wa[:, kc, :], hq[:, kc, :], kc == 0, kc == 7), reads=[rwa, rhq], writes=[rpsa])
                            for kc in range(8):
                                op("pe", MM(psb[:], wb[:, kc, :], hq[:, kc, :], kc == 0, kc == 7), reads=[rwb, rhq], writes=[rpsb])
                            a2, ra2 = a2s.next()
                            th, rth = tmps.next()
                            op("act", ACT(th[:], psb[:], AF.Tanh, scale=0.5), reads=[rpsb], writes=[rth])
                            op("dve", STT(a2[:, 15:15 + TW], th[:], 1.0, psa[:], ALU.add, ALU.mult), reads=[rth, rpsa], writes=[ra2])
                            if has_r:
                                psh, rpsh = psums.next()
                                for (w_, off) in ((wa, 0), (wb, 30)):
                                    for kc in range(8):
                                        op("pe", MM(psh[:, off + 15:off + 30], w_[:, kc, :], hh[:, kc, 15:30], kc == 0, kc == 7),
                                           reads=[rwa, rwb, rhh], writes=[rpsh])
                                thh, rthh = tmps.next()
                                op("act", ACT(thh[:, 15:30], psh[:, 45:60], AF.Tanh, scale=0.5), reads=[rpsh], writes=[rthh])
                                op("dve", STT(a2[:, 15 + TW:30 + TW], thh[:, 15:30], 1.0, psh[:, 15:30], ALU.add, ALU.mult), reads=[rthh, rpsh], writes=[ra2])
                            else:
                                op("dve", MS(a2[:, 15 + TW:30 + TW], 0.0), writes=[ra2])
                            if has_l:
                                op("dve", CP(a2[:, 0:15], keep[:, c, 0:15]), reads=[rkeep[c]], writes=[ra2])
                            else:
                                op("dve", MS(a2[:, 0:15], 0.0), writes=[ra2])
                            if has_r:
                                op("dve", CP(keep[:, c, 0:15], a2[:, TW:TW + 15]), reads=[ra2], writes=[rkeep[c]])
                            return a2, ra2

                        def emit_diag(c):
                            ident_bc = bass.AP(ident_b, 0, [[128, 128], [0, CONV_K], [1, 128]])
                            w_bc = bass.AP(dp, 64 + c * CONV_K, [[320, 128], [1, CONV_K], [0, 128]])
                            op("dve", TT(diag[:], ident_bc, w_bc, ALU.mult), reads=[r_const, r_dp], writes=[rdiag])

                        def emit_conv(c, a2, ra2):
                            psu, rpsu = psums.next()
                            for k in range(CONV_K):
                                op("pe", MM(psu[:], diag[:, k, :], a2[:, k:k + TW], k == 0, k == CONV_K - 1), reads=[rdiag, ra2], writes=[rpsu])
                            op("act", ACT(FK[:, c, :], psu[:], AF.Identity, bias=pp[:, C_CONVB + c:C_CONVB + c + 1]), reads=[rpsu, r_pp], writes=[rFK[c]])

                        for g in range(8):
                            wsv, rwsv = load_w(w_in[li, 48 + g])
                            ps, rps = psums.next()
                            for kt in range(4):
                                for kc in range(8):
                                    op("pe", MM(ps[:, kt * 128:(kt + 1) * 128], hq[:, kc, kt * 128:(kt + 1) * 128], wsv[:, kc, :], kc == 0, kc == 7),
                                       reads=[rwsv, rhq], writes=[rps])
                            op("act", ACT(GELv[:, :, g * 128:(g + 1) * 128], ps[:].rearrange("p (k c) -> p k c", k=4), AF.Gelu),
                               reads=[rps], writes=rFK)
                        nxt = emit_glu(0)
                        emit_diag(0)
                        op("dve", MS(st4[:], 0.0), writes=[rst4])
                        op("dve", RS(st4[:, 0:4], GELv), reads=rFK, writes=[rst4])
                        for kt in range(4):
                            for hf in range(2):
                                jk, rjk = tmps.next()
                                op("act", ACT(jk[:], GELv[:, kt, hf * 512:(hf + 1) * 512], AF.Square, accum_out=st4[:, 16 + kt * 2 + hf:17 + kt * 2 + hf]),
                                   reads=rFK + [rst4], writes=[rjk, rst4])
                        op("dve", RS(st4[:, 4:8], st4[:, 16:24].rearrange("p (k h) -> p k h", h=2)), reads=[rst4], writes=[rst4])
                        op("dve", TS(st4[:, 8:12], st4[:, 0:4], 1.0 / D, None, ALU.mult), reads=[rst4], writes=[rst4])
                        op("dve", TT(st4[:, 12:16], st4[:, 8:12], st4[:, 8:12], ALU.mult), reads=[rst4], writes=[rst4])
                        op("dve", STT(st4[:, 4:8], st4[:, 4:8], 1.0 / D, st4[:, 12:16], ALU.mult, ALU.subtract), reads=[rst4], writes=[rst4])
                        op("act", ACT(st4[:, 12:16], st4[:, 4:8], AF.Ln, bias=epsc[:, 0:1]), reads=[rst4, r_const], writes=[rst4])
                        op("act", ACT(st4[:, 4:8], st4[:, 12:16], AF.Exp, scale=-0.5), reads=[rst4], writes=[rst4])
                        for kt in range(4):
                            op("dve", TS(GVv[:, kt, :], GELv[:, kt, :], st4[:, 8 + kt:9 + kt], st4[:, 4 + kt:5 + kt], ALU.subtract, ALU.mult),
                               reads=rFK + [rst4], writes=rMT)
                        for c in range(8):
                            cur = nxt
                            if c + 1 < 8:
                                nxt = emit_glu(c + 1)
                            emit_conv(c, *cur)
                            if c + 1 < 8:
                                emit_diag(c + 1)
                        ps1, rps1 = psums.next()
                        for c in range(8):
                            op("pe", MM(ps1[:], ones_f[:], FK[:, c, :], c == 0, c == 7), reads=[rFK[c], r_const], writes=[rps1])
                        ps2, rps2 = sumsq_bcast([(FK[:, c, :], [rFK[c]]) for c in range(8)])
                        mean, rmean = L1, rL1
                        op("dve", TS(mean[:], ps1[:], 1.0 / D, None, ALU.mult), reads=[rps1], writes=[rmean])
                        msq, rmsq = tmps.next()
                        op("dve", TT(msq[:], mean[:], mean[:], ALU.mult), reads=[rmean], writes=[rmsq])
                        var, rvar = tmps.next()
                        op("dve", STT(var[:], ps2[:], 1.0 / D, msq[:], ALU.mult, ALU.subtract), reads=[rps2, rmsq], writes=[rvar])
                        rs, rrs = L2, rL2
                        t1_, r1_ = tmps.next()
                        op("act", ACT(t1_[:], var[:], AF.Ln, bias=epsc[:, 0:1]), reads=[rvar, r_const], writes=[r1_])
                        op("act", ACT(rs[:], t1_[:], AF.Exp, scale=-0.5), reads=[r1_], writes=[rrs])
                        for c in range(8):
                            ta, rta = tmps.next()
                            op("dve", TT(ta[:], FK[:, c, :], mean[:], ALU.subtract), reads=[rFK[c], rmean], writes=[rta])
                            tb, rtb = tmps.next()
                            op("dve", TT(tb[:], ta[:], rs[:], ALU.mult), reads=[rta, rrs], writes=[rtb])
                            yh, ryh = tmps.next()
                            op("dve", TS(yh[:], tb[:], dp[:, 24 + c:25 + c], dp[:, 32 + c:33 + c], ALU.mult, ALU.add), reads=[rtb, r_dp], writes=[ryh])
                            th, rth = tmps.next()
                            op("act", ACT(th[:], yh[:], AF.Tanh), reads=[ryh], writes=[rth])
                            op("dve", STT(A[:, c, :], th[:], 1.0, yh[:], ALU.add, ALU.mult), reads=[rth, ryh], writes=[rA[c]])

                        for g in range(8):
                            wsu, rwsu = load_w(w_in[li, 40 + g])
                            psu_, rpsu_ = psums.next()
                            for kc in range(8):
                                op("pe", MM(psu_[:], wsu[:, kc, :], hq[:, kc, :], kc == 0, kc == 7), reads=[rwsu, rhq], writes=[rpsu_])
                            gu, rgu = tmps.next()
                            op("act", ACT(gu[:], psu_[:], AF.Gelu), reads=[rpsu_], writes=[rgu])
                            psm, rpsm = psums.next()
                            for n in range(4):
                                op("pe", MM(psm[:, n * 128:(n + 1) * 128], GVv[:, n, g * 128:(g + 1) * 128], wsb[:, g, :], True, True),
                                   reads=rMT + [rwsb], writes=[rpsm])
                            mx, rmx = tmps.next()
                            bp_bc = bass.AP(Bp, g * 128, [[1024, 128], [0, 4], [1, 128]])
                            op("dve", STT(mx[:].rearrange("p (n t) -> p n t", n=4), psm[:].rearrange("p (n t) -> p n t", n=4),
                                          pp[:, C_SGUG + g:C_SGUG + g + 1], bp_bc, ALU.mult, ALU.add),
                               reads=[rpsm, rBp, r_pp], writes=[rmx])
                            op("dve", TT(Y[:, g, :], gu[:], mx[:], ALU.mult), reads=[rgu, rmx], writes=[rY[g]])

                        for j in range(8):
                            terms = []
                            for (wp_d, src, rsrc, gch, hbcol) in ((w_pa, A, rA, 56 + j, j), (w_pb, None, None, 64 + j, 8 + j), (w_pc, Y, rY, 72 + j, 16 + j)):
                                wp, rwp = load_w(wp_d[li, j])
                                wg, rwg = load_w(w_in[li, gch])
                                pp_, rpp_ = psums.next()
                                pg_, rpg_ = psums.next()
                                for c in range(8):
                                    if src is None:
                                        op("pe", MM(pp_[:], wp[:, c, :], OT[:, c, sl], c == 0, c == 7), reads=[rwp, rOT[c][tt]], writes=[rpp_])
                                    else:
                                        op("pe", MM(pp_[:], wp[:, c, :], src[:, c, :], c == 0, c == 7), reads=[rwp, rsrc[c]], writes=[rpp_])
                                for kc in range(8):
                                    op("pe", MM(pg_[:], wg[:, kc, :], hq[:, kc, :], kc == 0, kc == 7), reads=[rwg, rhq], writes=[rpg_])
                                th, rth = tmps.next()
                                op("act", ACT(th[:], pg_[:], AF.Tanh, bias=dp[:, hbcol:hbcol + 1], scale=0.5), reads=[rpg_, r_dp], writes=[rth])
                                m_, rm_ = tmps.next()
                                op("dve", STT(m_[:], th[:], 1.0, pp_[:], ALU.add, ALU.mult), reads=[rth, rpp_], writes=[rm_])
                                terms.append((m_, rm_))
                            s_, rs_ = tmps.next()
                            op("dve", TT(s_[:], terms[0][0][:], terms[1][0][:], ALU.add), reads=[terms[0][1], terms[1][1]], writes=[rs_])
                            op("dve", TT(MT[:, j, :], s_[:], terms[2][0][:], ALU.add), reads=[rs_, terms[2][1]], writes=[rMT[j]])

                        for i in range(8):
                            wo, rwo = load_w(w_o[li, i])
                            ps, rps = psums.next()
                            for j in range(8):
                                op("pe", MM(ps[:], wo[:, j, :], MT[:, j, :], j == 0, j == 7), reads=[rwo, rMT[j]], writes=[rps])
                            op("act", ACT(FK[:, i, :], ps[:], AF.Copy), reads=[rps], writes=[rFK[i]])
                        ss, rss = sumsq_bcast([(FK[:, i, :], [rFK[i]]) for i in range(8)])
                        rb, rrb = L0, rL0
                        rstd_bcast(ss[:], rss, 1.0 / D, 4.0 * EPS, rb[:], rrb)
                        for i in range(8):
                            ta, rta = tmps.next()
                            op("dve", STT(ta[:], FK[:, i, :], pp[:, C_GPOST + i:C_GPOST + i + 1], rb[:], ALU.mult, ALU.mult), reads=[rFK[i], rrb, r_pp], writes=[rta])
                            op("dve", TT(xT[:, i, sl], xT[:, i, sl], ta[:], ALU.add), reads=[rx[i][tt], rta], writes=[rx[i][tt]])
                P.fence()

            with ExitStack() as st:
                h2 = sb(st, "h2", [128, 8, TW], BF16); rh2 = Res()
                hid = sb(st, "hid", [128, 32, TW], BF16); rhid = [Res() for _ in range(32)]
                Fb = sb(st, "Fb", [128, 8, TW], F32); rFb = [Res() for _ in range(8)]
                wdp = Pool([sb(st, f"wd{i}", [128, 32, 128], BF16) for i in range(2)])
                for tt in range(NQ):
                    sl = slice(tt * TW, (tt + 1) * TW)
                    ss, rss = sumsq_bcast([(xT[:, c, sl], [rx[c][tt]]) for c in range(8)])
                    rb, rrb = L0, rL0
                    rstd_bcast(ss[:], rss, 1.0 / D, EPS, rb[:], rrb)
                    for c in range(8):
                        op("dve", STT(h2[:, c, :], xT[:, c, sl], pp[:, C_FPRE + c:C_FPRE + c + 1], rb[:], ALU.mult, ALU.mult),
                           reads=[rx[c][tt], rrb, r_pp], writes=[rh2])
                    for m in range(32):
                        wu, rwu = load_w(w_up[li, m])
                        ps, rps = psums.next()
                        for kc in range(8):
                            op("pe", MM(ps[:], wu[:, kc, :], h2[:, kc, :], kc == 0, kc == 7), reads=[rwu, rh2], writes=[rps])
                        rl, rrl = tmps.next()
                        op("act", ACT(rl[:], ps[:], AF.Relu), reads=[rps], writes=[rrl])
                        op("dve", TT(hid[:, m, :], rl[:], rl[:], ALU.mult), reads=[rrl], writes=[rhid[m]])
                    for i in range(8):
                        wd, rwd = wdp.next()
                        op("pool", DMA(wd[:], w_dn[li, i]), writes=[rwd], dma=True)
                        ps, rps = psums.next()
                        for m in range(32):
                            op("pe", MM(ps[:], wd[:, m, :], hid[:, m, :], m == 0, m == 31), reads=[rwd, rhid[m]], writes=[rps])
                        op("act", ACT(Fb[:, i, :], ps[:], AF.Copy), reads=[rps], writes=[rFb[i]])
                    ss, rss = sumsq_bcast([(Fb[:, i, :], [rFb[i]]) for i in range(8)])
                    rb, rrb = L0, rL0
                    rstd_bcast(ss[:], rss, 1.0 / D, EPS, rb[:], rrb)
                    for i in range(8):
                        ta, rta = tmps.next()
                        op("dve", STT(ta[:], Fb[:, i, :], pp[:, C_FPOST + i:C_FPOST + i + 1], rb[:], ALU.mult, ALU.mult), reads=[rFb[i], rrb, r_pp], writes=[rta])
                        op("dve", TT(xT[:, i, sl], xT[:, i, sl], ta[:], ALU.add), reads=[rx[i][tt], rta], writes=[rx[i][tt]])
            P.fence()
            if dbg and li == 0:
                for c in range(8):
                    op("sp", DMA(dbg_out[c * 128:(c + 1) * 128, :], xT[:, c, :]), reads=rx[c], dma=True, is_out=True)

        for c in range(8):
            op("sp", DMA(yout[c * 128:(c + 1) * 128, :], xT[:, c, :]), reads=rx[c], dma=True, is_out=True)
        P.emit()
    return nc


def _tile_w(w, kc):
    K, N = w.shape
    return np.ascontiguousarray(w.reshape(kc, 128, N // 128, 128).transpose(2, 1, 0, 3))


def _fm(v):
    return np.ascontiguousarray(v.reshape(-1, 128).T)


def _prep_layer_inputs(inp, layers):
    f = lambda a: np.asarray(a, dtype=np.float32)
    out = {}
    out["w_in"] = np.stack([_tile_w(f(inp["w_in"][l]), 8) for l in layers])
    out["w_pa"] = np.stack([_tile_w(f(inp["w_proj_conv"][l]), 8) for l in layers])
    out["w_pb"] = np.stack([_tile_w(f(inp["w_proj_attn"][l]), 8) for l in layers])
    out["w_pc"] = np.stack([_tile_w(f(inp["w_proj_sgu"][l]), 8) for l in layers])
    out["w_o"] = np.stack([_tile_w(f(inp["w_out"][l]), 8) for l in layers])
    out["w_up"] = np.stack([_tile_w(f(inp["w_ffn_up"][l]), 8) for l in layers])
    out["w_dn"] = np.stack([_tile_w(f(inp["w_ffn_down"][l]), 32) for l in layers])
    out["wsT"] = np.stack([np.ascontiguousarray(f(inp["sgu_w"][l]).transpose(2, 0, 1)) for l in layers])
    out["sgub"] = np.stack([f(inp["sgu_b"][l]).reshape(1, 1024) for l in layers])
    pps = []
    for l in layers:
        p = np.zeros((128, NPP2), np.float32)
        p[:, C_GPRE:C_GPRE + 8] = _fm(f(inp["norm_mix_pre"][l]))
        p[:, C_GPOST:C_GPOST + 8] = _fm(f(inp["norm_mix_post"][l]))
        p[:, C_CONVB:C_CONVB + 8] = _fm(f(inp["conv_b"][l]))
        p[:, C_CLNG:C_CLNG + 8] = _fm(f(inp["conv_ln_g"][l]))
        p[:, C_CLNB:C_CLNB + 8] = _fm(f(inp["conv_ln_b"][l]))
        p[:, C_BGATE:C_BGATE + 24] = _fm(f(inp["b_gate"][l]))
        p[:, C_SUBG] = f(inp["subln_g"][l])
        p[:, C_FPRE:C_FPRE + 8] = _fm(f(inp["norm_ffn_pre"][l]))
        p[:, C_FPOST:C_FPOST + 8] = _fm(f(inp["norm_ffn_post"][l]))
        cw = f(inp["conv_w"][l])
        p[:, C_CONVW:C_CONVW + 248] = cw.T.reshape(8, 128, 31).transpose(1, 0, 2).reshape(128, 248)
        lam = np.concatenate([f(inp[k][l]) for k in ("lam_q1", "lam_k1", "lam_q2", "lam_k2")])
        p[:, C_LAM:C_LAM + 256] = np.broadcast_to(lam[None, :], (128, 256))
        p[:, C_SGUG:C_SGUG + 8] = _fm(f(inp["sgu_ln_g"][l]))
        p[:, C_SGUB:C_SGUB + 8] = _fm(f(inp["sgu_ln_b"][l]))
        pps.append(p)
    out["pp"] = np.stack(pps)
    return out


def _const_tables():
    pidx = np.arange(128, dtype=np.float64)[:, None]
    sstrip = np.abs(np.arange(896, dtype=np.float64)[None, :] - pidx - 384.0).astype(np.float32)
    lin = np.broadcast_to(np.arange(512, dtype=np.float32)[None, :], (128, 512)).copy()
    btab = np.zeros((128, 8, 28), np.float32)
    for h in range(8):
        slope = 2.0 ** (-(h + 1))
        for di in range(28):
            Dd = di * 128 - 1920
            btab[:, h, di] = (-slope * np.abs(Dd - pidx[:, 0])).astype(np.float32)
    return {"ident": np.eye(128, dtype=np.float32), "sstrip": sstrip, "lin": lin,
            "btab": btab.reshape(128, 8 * 28)}


_NC_CACHE = {}


def _run(xT_list, inp, layers):
    key = tuple(layers)
    if key not in _NC_CACHE:
        _NC_CACHE[key] = build_nc(layers)
    nc = _NC_CACHE[key]
    shared = _prep_layer_inputs(inp, layers)
    shared.update(_const_tables())
    in_maps = []
    for b in range(8):
        m = dict(shared)
        m["xT"] = xT_list[b]
        in_maps.append(m)
    res = run_bass_kernel_spmd(nc, in_maps, core_ids=list(range(8)))
    return [np.asarray(r["yT"], dtype=np.float32) for r in res.results]


def kernel(**inputs):
    x = np.asarray(inputs["x"], dtype=np.float32)
    xT = [np.ascontiguousarray(x[b].T) for b in range(8)]
    if FUSED:
        yT = _run(xT, inputs, [0, 1])
    else:
        yT = _run(xT, inputs, [0])
        yT = _run([np.ascontiguousarray(a) for a in yT], inputs, [1])
    return np.stack([a.T for a in yT]).astype(np.float32)
```

```python
import math
from contextlib import ExitStack

import numpy as np
import concourse.bass as bass
import concourse.mybir as mybir
from concourse.bass_utils import run_bass_kernel_spmd

F32 = mybir.dt.float32
BF16 = mybir.dt.bfloat16
ALU = mybir.AluOpType
AF = mybir.ActivationFunctionType

FUSED = True

DEPTH = 2
T = 2048
D = 1024
TW = 512
NQ = 4
EPS = 1e-6
CONV_K = 31
NPP = 585
C_GPRE, C_GPOST, C_CONVB, C_CLNG, C_CLNB, C_BGATE, C_SUBG, C_FPRE, C_FPOST, C_CONVW, C_LAM = \
    0, 8, 16, 24, 32, 40, 64, 65, 73, 81, 329
C_SGUG, C_SGUB = 585, 593
NPP2 = 601

ENGS = ("pe", "act", "dve", "pool", "sp")
NROTS = {"pe": 16, "act": 4, "dve": 4, "pool": 2, "sp": 1}
LAST_READER_ONLY = True
NDMAK = {"sw": 48, "hw": 8}


class Res:
    __slots__ = ("name", "w", "rs")

    def __init__(self, name=""):
        self.name = name
        self.w = None
        self.rs = {}


class Op:
    __slots__ = ("eng", "fn", "deps", "signal", "sidx", "is_dma", "didx", "dkind")


class Prog:
    def __init__(self, nc):
        self.nc = nc
        self.streams = {e: [] for e in ENGS}
        self.dmas = {"sw": [], "hw": []}
        self.out_dmas = []
        self.pending = {e: [] for e in ENGS}
        self.dma_since_fence = []

    def fence(self):
        lasts = []
        for e in ENGS:
            for o in reversed(self.streams[e]):
                if not o.is_dma:
                    lasts.append(o)
                    break
        lasts += self.dma_since_fence
        self.dma_since_fence = []
        for e in ENGS:
            self.pending[e] = list(lasts)

    def op(self, eng, fn, reads=(), writes=(), dma=False, is_out=False):
        o = Op()
        o.eng = eng
        o.fn = fn
        o.signal = False
        o.sidx = 0
        o.is_dma = dma
        o.didx = -1
        o.dkind = "sw" if eng == "pool" else "hw"
        deps = {}
        for r in reads:
            if r.w is not None:
                deps[r.w] = True
        for w in writes:
            if w.w is not None and w.w not in deps:
                deps[w.w] = False
            for rd in w.rs.values():
                if rd not in deps:
                    deps[rd] = False
        final = []
        for d, raw in deps.items():
            if not dma and not d.is_dma and d.eng == eng:
                if eng == "pe":
                    continue
            final.append(d)
        if self.pending[eng]:
            for d in self.pending[eng]:
                if d not in deps and not (d.eng == eng and not d.is_dma and not dma):
                    final.append(d)
            self.pending[eng] = []
        if dma:
            lst = self.dmas[o.dkind]
            o.didx = len(lst)
            if o.didx >= NDMAK[o.dkind]:
                final.append(lst[o.didx - NDMAK[o.dkind]])
            lst.append(o)
            self.dma_since_fence.append(o)
            if is_out:
                self.out_dmas.append(o)
        o.deps = final
        for d in final:
            if not d.is_dma:
                d.signal = True
        rkey = ("dma", o.dkind, o.didx) if dma else (eng if LAST_READER_ONLY else id(o))
        for r in reads:
            r.rs[rkey] = o
        for w in writes:
            w.w = o
            w.rs = {}
        self.streams[eng].append(o)
        return o

    def emit(self, final_engine="sp"):
        nc = self.nc
        for e in ENGS:
            c = 0
            for o in self.streams[e]:
                if not o.is_dma and o.signal:
                    o.sidx = c
                    c += 1
        with ExitStack() as st:
            esem = {e: [st.enter_context(nc.semaphore(f"s_{e}{i}")) for i in range(NROTS[e])] for e in ENGS}
            dsem = {k: [st.enter_context(nc.semaphore(f"s_dma{k}{i}")) for i in range(n)] for k, n in NDMAK.items()}
            block = st.enter_context(nc.Block())
            out_dmas = self.out_dmas
            streams = self.streams

            def run(ename, eng):
                waited = {e: -1 for e in ENGS}
                dwaited = {}

                def dwait(d):
                    n = NDMAK[d.dkind]
                    s = (d.dkind, d.didx % n)
                    v = 16 * (d.didx // n + 1)
                    if dwaited.get(s, 0) < v:
                        eng.wait_ge(dsem[s[0]][s[1]], v)
                        dwaited[s] = v

                for o in streams[ename]:
                    for d in o.deps:
                        if d.is_dma:
                            dwait(d)
                        else:
                            if waited[d.eng] < d.sidx:
                                eng.wait_ge(esem[d.eng][d.sidx % NROTS[d.eng]], d.sidx // NROTS[d.eng] + 1)
                                waited[d.eng] = d.sidx
                    inst = o.fn(eng)
                    if o.is_dma:
                        inst.then_inc(dsem[o.dkind][o.didx % NDMAK[o.dkind]], 16)
                    elif o.signal:
                        inst.then_inc(esem[ename][o.sidx % NROTS[ename]], 1)
                if ename == final_engine:
                    for d in out_dmas:
                        dwait(d)

            @block.tensor
            def _(eng):
                run("pe", eng)

            @block.scalar
            def _(eng):
                run("act", eng)

            @block.vector
            def _(eng):
                run("dve", eng)

            @block.gpsimd
            def _(eng):
                run("pool", eng)

            @block.sync
            def _(eng):
                run("sp", eng)


class Pool:
    def __init__(self, tiles, res=None):
        self.tiles = tiles
        self.res = res if res is not None else [Res() for _ in tiles]
        self.i = 0

    def next(self):
        k = self.i % len(self.tiles)
        self.i += 1
        return self.tiles[k], self.res[k]


def build_nc(layers, dbg=False):
    nc = bass.Bass("TRN2", target_bir_lowering=False)
    nl = len(layers)
    xin = nc.dram_tensor("xT", [D, T], F32, kind="ExternalInput").ap()
    yout = nc.dram_tensor("yT", [D, T], F32, kind="ExternalOutput").ap()
    w_in = nc.dram_tensor("w_in", [nl, 80, 128, 8, 128], F32, kind="ExternalInput").ap()
    w_pa = nc.dram_tensor("w_pa", [nl, 8, 128, 8, 128], F32, kind="ExternalInput").ap()
    w_pb = nc.dram_tensor("w_pb", [nl, 8, 128, 8, 128], F32, kind="ExternalInput").ap()
    w_pc = nc.dram_tensor("w_pc", [nl, 8, 128, 8, 128], F32, kind="ExternalInput").ap()
    w_o = nc.dram_tensor("w_o", [nl, 8, 128, 8, 128], F32, kind="ExternalInput").ap()
    w_up = nc.dram_tensor("w_up", [nl, 32, 128, 8, 128], F32, kind="ExternalInput").ap()
    w_dn = nc.dram_tensor("w_dn", [nl, 8, 128, 32, 128], F32, kind="ExternalInput").ap()
    wsT = nc.dram_tensor("wsT", [nl, 128, 8, 128], F32, kind="ExternalInput").ap()
    sgub = nc.dram_tensor("sgub", [nl, 1, 1024], F32, kind="ExternalInput").ap()
    ppd = nc.dram_tensor("pp", [nl, 128, NPP2], F32, kind="ExternalInput").ap()
    identd = nc.dram_tensor("ident", [128, 128], F32, kind="ExternalInput").ap()
    sstripd = nc.dram_tensor("sstrip", [128, 896], F32, kind="ExternalInput").ap()
    lind = nc.dram_tensor("lin", [128, 512], F32, kind="ExternalInput").ap()
    btabd = nc.dram_tensor("btab", [128, 8 * 28], F32, kind="ExternalInput").ap()

    dbg_out = nc.dram_tensor("dbg", [D, T], F32, kind="ExternalOutput").ap() if dbg else None
    P = Prog(nc)
    op = P.op

    def MM(out, lhsT, rhs, start, stop):
        return lambda e: e.matmul(out, lhsT, rhs, start=start, stop=stop)

    def ACT(out, in_, func, bias=0.0, scale=1.0, accum_out=None):
        if accum_out is None:
            return lambda e: e.activation(out, in_, func, bias=bias, scale=scale)
        return lambda e: e.activation(out, in_, func, bias=bias, scale=scale, accum_out=accum_out)

    def TS(out, in0, s1, s2, op0, op1=None):
        if op1 is None:
            return lambda e: e.tensor_scalar(out, in0, s1, None, op0)
        return lambda e: e.tensor_scalar(out, in0, s1, s2, op0, op1)

    def STT(out, in0, scalar, in1, op0, op1):
        return lambda e: e.scalar_tensor_tensor(out, in0, scalar, in1, op0, op1)

    def TT(out, in0, in1, op_):
        return lambda e: e.tensor_tensor(out, in0, in1, op_)

    def CP(out, in_):
        return lambda e: e.tensor_copy(out, in_)

    def DMA(out, in_):
        return lambda e: e.dma_start(out=out, in_=in_)

    def RS(out, in_):
        return lambda e: e.reduce_sum(out, in_, mybir.AxisListType.X)

    def MS(ap, val):
        return lambda e: e.memset(ap, val)

    def RCP(out, in_):
        return lambda e: e.reciprocal(out, in_)

    with ExitStack() as st0:
        uniq = [0]

        def sb(st, name, shape, dt):
            uniq[0] += 1
            return st.enter_context(nc.sbuf_tensor(f"{name}_{uniq[0]}", shape, dt))

        xT = sb(st0, "xT_sb", [128, 8, T], F32)
        rx = [[Res() for _ in range(NQ)] for _ in range(8)]
        ones_f = sb(st0, "ones_f", [128, 128], F32)
        ones_b = sb(st0, "ones_b", [128, 128], BF16)
        ident_b = sb(st0, "ident_b", [128, 128], BF16)
        r_const = Res()
        pp = sb(st0, "pp_sb", [128, NPP2], F32)
        r_pp = Res()
        dp = sb(st0, "dp_sb", [128, 320], F32)
        r_dp = Res()
        banks = [st0.enter_context(nc.psum_tensor(f"ps{i}", [128, 512], F32)) for i in range(8)]
        bres = [Res() for _ in range(8)]
        ps_all = Pool(banks, bres)
        psA = Pool(banks[:4], bres[:4])
        psB = Pool(banks[4:], bres[4:])

        class _Cur:
            pass
        psums = _Cur()
        psums.pool = ps_all
        psums.next = lambda: psums.pool.next()
        tmps = Pool([sb(st0, f"tmp{i}", [128, 512], F32) for i in range(8)])
        L0 = sb(st0, "L0", [128, 512], F32); rL0 = Res()
        L1 = sb(st0, "L1", [128, 512], F32); rL1 = Res()
        L2 = sb(st0, "L2", [128, 512], F32); rL2 = Res()
        wpool = Pool([sb(st0, f"w{i}", [128, 8, 128], BF16) for i in range(6)])

        op("dve", MS(ones_f[:], 1.0), writes=[r_const])
        op("dve", MS(ones_b[:], 1.0), writes=[r_const])
        op("pool", DMA(ident_b[:], identd), writes=[r_const], dma=True)
        for c in range(8):
            op("sp", DMA(xT[:, c, :], xin[c * 128:(c + 1) * 128, :]), writes=rx[c], dma=True)

        def load_w(src):
            t, r = wpool.next()
            op("pool", DMA(t[:], src), writes=[r], dma=True)
            return t, r

        def rstd_bcast(ss_ps, r_ss, scale, eps, out_ap, r_out, n=TW):
            t1, r1 = tmps.next()
            op("act", ACT(t1[:, 0:n], ss_ps, AF.Ln, bias=eps_ap(eps), scale=scale), reads=[r_ss, r_const], writes=[r1])
            op("act", ACT(out_ap, t1[:, 0:n], AF.Exp, scale=-0.5), reads=[r1], writes=[r_out])

        epsc = sb(st0, "epsc", [128, 2], F32)
        op("dve", MS(epsc[:, 0:1], EPS), writes=[r_const])
        op("dve", MS(epsc[:, 1:2], 4.0 * EPS), writes=[r_const])

        def eps_ap(eps):
            return epsc[:, 0:1] if eps == EPS else epsc[:, 1:2]

        def sumsq_bcast(srcs, n=TW):
            ps, rps = psums.next()
            k = len(srcs)
            for i, (ap, rr) in enumerate(srcs):
                sq, rsq = tmps.next()
                op("act", ACT(sq[:, 0:n], ap, AF.Square), reads=rr, writes=[rsq])
                op("pe", MM(ps[:, 0:n], ones_f[:], sq[:, 0:n], i == 0, i == k - 1), reads=[rsq, r_const], writes=[rps])
            return ps, rps

        for li, l in enumerate(layers):
            lam_init = 0.8 - 0.6 * math.exp(-0.3 * l)
            op("sp", DMA(pp[:], ppd[li]), writes=[r_pp], dma=True)
            op("dve", TS(dp[:, 0:24], pp[:, C_BGATE:C_BGATE + 24], 0.5, None, ALU.mult), reads=[r_pp], writes=[r_dp])
            op("dve", TS(dp[:, 24:40], pp[:, C_CLNG:C_CLNG + 16], 0.5, None, ALU.mult), reads=[r_pp], writes=[r_dp])
            op("dve", TS(dp[:, 40:41], pp[:, C_SUBG:C_SUBG + 1], 1.0 - lam_init, None, ALU.mult), reads=[r_pp], writes=[r_dp])
            op("dve", TS(dp[:, 64:312], pp[:, C_CONVW:C_CONVW + 248], 0.5, None, ALU.mult), reads=[r_pp], writes=[r_dp])
            lt, rlt = tmps.next()
            op("dve", TT(lt[:, 0:64], pp[:, C_LAM:C_LAM + 64], pp[:, C_LAM + 64:C_LAM + 128], ALU.mult), reads=[r_pp], writes=[rlt])
            op("dve", TT(lt[:, 64:128], pp[:, C_LAM + 128:C_LAM + 192], pp[:, C_LAM + 192:C_LAM + 256], ALU.mult), reads=[r_pp, rlt], writes=[rlt])
            op("dve", RS(dp[:, 42:43], lt[:, 0:64]), reads=[rlt], writes=[r_dp])
            op("dve", RS(dp[:, 43:44], lt[:, 64:128]), reads=[rlt], writes=[r_dp])
            op("act", ACT(dp[:, 44:46], dp[:, 42:44], AF.Exp), reads=[r_dp], writes=[r_dp])
            op("dve", STT(dp[:, 41:42], dp[:, 45:46], -lam_init, dp[:, 44:45], ALU.add, ALU.subtract), reads=[r_dp], writes=[r_dp])

            with ExitStack() as st1:
                OT = sb(st1, "OT", [128, 8, T], BF16)
                rOT = [[Res() for _ in range(NQ)] for _ in range(8)]
                with ExitStack() as st:
                    hT = sb(st, "hT", [128, 8, T], BF16)
                    rh = [[Res() for _ in range(NQ)] for _ in range(8)]
                    QT = sb(st, "QT", [128, T], BF16); rQ = [Res() for _ in range(NQ)]
                    KT = sb(st, "KT", [128, T], BF16); rK = [Res() for _ in range(NQ)]
                    VH = sb(st, "VH", [128, 16, 128], BF16); rV = [Res() for _ in range(4)]
                    sstrip = sb(st, "sstrip_sb", [128, 896], F32)
                    lin = sb(st, "lin_sb", [128, 512], F32)
                    btab = sb(st, "btab_sb", [128, 8 * 28], F32)
                    r_tab = Res()
                    epool = Pool([sb(st, f"E{i}", [128, 512], BF16) for i in range(8)])
                    tbp = Pool([sb(st, f"tb{i}", [128, 512], F32) for i in range(6)])
                    op("sp", DMA(sstrip[:], sstripd), writes=[r_tab], dma=True)
                    op("sp", DMA(lin[:], lind), writes=[r_tab], dma=True)
                    op("sp", DMA(btab[:], btabd), writes=[r_tab], dma=True)

                    psums.pool = psA
                    for tt in range(NQ):
                        sl = slice(tt * TW, (tt + 1) * TW)
                        ss, rss = sumsq_bcast([(xT[:, c, sl], [rx[c][tt]]) for c in range(8)])
                        rb, rrb = L0, rL0
                        rstd_bcast(ss[:], rss, 1.0 / D, EPS, rb[:], rrb)
                        for c in range(8):
                            op("dve", STT(hT[:, c, sl], xT[:, c, sl], pp[:, C_GPRE + c:C_GPRE + c + 1], rb[:], ALU.mult, ALU.mult),
                               reads=[rx[c][tt], rrb, r_pp], writes=[rh[c][tt]])

                    for h in range(8):
                        slope = 2.0 ** (-(h + 1))
                        wq, rwq = load_w(w_in[li, 16 + h])
                        wk, rwk = load_w(w_in[li, 24 + h])
                        wv, rwv = load_w(w_in[li, 32 + h])
                        for (w_, rw_, dst, rdst) in ((wq, rwq, QT, rQ), (wk, rwk, KT, rK)):
                            for tt in range(NQ):
                                sl = slice(tt * TW, (tt + 1) * TW)
                                ps, rps = psums.next()
                                for kc in range(8):
                                    op("pe", MM(ps[:], w_[:, kc, :], hT[:, kc, sl], kc == 0, kc == 7),
                                       reads=[rw_, rh[kc][tt]], writes=[rps])
                                op("act", ACT(dst[:, sl], ps[:], AF.Copy), reads=[rps], writes=[rdst[tt]])
                        for kg in range(4):
                            ps, rps = psums.next()
                            for kq in range(4):
                                kt = kg * 4 + kq
                                for kc in range(8):
                                    op("pe", MM(ps[:, kq * 128:(kq + 1) * 128], hT[:, kc, kt * 128:(kt + 1) * 128], wv[:, kc, :], kc == 0, kc == 7),
                                       reads=[rwv, rh[kc][kg]], writes=[rps])
                            op("dve", CP(VH[:, kg * 4:(kg + 1) * 4, :], ps[:].rearrange("p (k e) -> p k e", k=4)), reads=[rps], writes=[rV[kg]])
                        for qt in range(NQ):
                            qsl = slice(qt * TW, (qt + 1) * TW)
                            acc = [psB.next() for _ in range(4)]
                            def emit_S(kt, h=h, qt=qt, qsl=qsl, slope=slope):
                                ksl = slice(kt * 128, (kt + 1) * 128)
                                Dd = qt * TW - kt * 128
                                Es = []
                                for j in range(2):
                                    psl = slice(j * 64, (j + 1) * 64)
                                    ps, rps = psums.next()
                                    op("pe", MM(ps[:], KT[psl, ksl], QT[psl, qsl], True, True),
                                       reads=[rK[kt // 4], rQ[qt]], writes=[rps])
                                    tb, rtb = tbp.next()
                                    E, rE = epool.next()
                                    if -512 < Dd < 128:
                                        x0 = Dd + 384
                                        op("dve", STT(tb[:], sstrip[:, x0:x0 + 512], -8.0 * slope, ps[:], ALU.mult, ALU.add),
                                           reads=[rps, r_tab], writes=[rtb])
                                        op("act", ACT(E[:], tb[:], AF.Exp, scale=0.125), reads=[rtb], writes=[rE])
                                    else:
                                        sgn = -8.0 * slope if Dd >= 128 else 8.0 * slope
                                        di = (Dd + 1920) // 128
                                        op("dve", STT(tb[:], lin[:], sgn, ps[:], ALU.mult, ALU.add),
                                           reads=[rps, r_tab], writes=[rtb])
                                        op("act", ACT(E[:], tb[:], AF.Exp, bias=btab[:, h * 28 + di:h * 28 + di + 1], scale=0.125),
                                           reads=[rtb, r_tab], writes=[rE])
                                    Es.append((E, rE))
                                return Es

                            LA = 3
                            Eq = [emit_S(k) for k in range(LA)]
                            for kt in range(16):
                                Es = Eq.pop(0)
                                if kt + LA < 16:
                                    Eq.append(emit_S(kt + LA))
                                for j in range(2):
                                    E, rE = Es[j]
                                    op("pe", MM(acc[j][0][:], VH[:, kt, :], E[:], kt == 0, kt == 15),
                                       reads=[rV[kt // 4], rE], writes=[acc[j][1]])
                                    op("pe", MM(acc[2 + j][0][:], ones_b[:], E[:], kt == 0, kt == 15),
                                       reads=[r_const, rE], writes=[acc[2 + j][1]])
                            R0, rR0 = tmps.next(); R1, rR1 = tmps.next()
                            for (R_, rR_, zi) in ((R0, rR0, 2), (R1, rR1, 3)):
                                lz, rlz = tmps.next()
                                op("act", ACT(lz[:], acc[zi][0][:], AF.Ln), reads=[acc[zi][1]], writes=[rlz])
                                op("act", ACT(R_[:], lz[:], AF.Exp, scale=-1.0), reads=[rlz], writes=[rR_])
                            t0, rt0 = tmps.next(); t1, rt1 = tmps.next()
                            op("dve", TT(t0[:], acc[0][0][:], R0[:], ALU.mult), reads=[acc[0][1], rR0], writes=[rt0])
                            op("dve", TT(t1[:], acc[1][0][:], R1[:], ALU.mult), reads=[acc[1][1], rR1], writes=[rt1])
                            oo, roo = tmps.next()
                            op("dve", STT(oo[:], t1[:], dp[:, 41:42], t0[:], ALU.mult, ALU.add), reads=[rt0, rt1, r_dp], writes=[roo])
                            ss, rss = sumsq_bcast([(oo[:], [roo])])
                            rb, rrb = L0, rL0
                            rstd_bcast(ss[:], rss, 1.0 / 128.0, EPS, rb[:], rrb)
                            op("dve", STT(OT[:, h, qsl], oo[:], dp[:, 40:41], rb[:], ALU.mult, ALU.mult),
                               reads=[roo, rrb, r_dp], writes=[rOT[h][qt]])
                P.fence()
                psums.pool = ps_all

                with ExitStack() as st:
                    hq = sb(st, "hq", [128, 8, TW], BF16); rhq = Res()
                    hh = sb(st, "hh", [128, 8, 32], BF16); rhh = Res()
                    keep = sb(st, "keep", [128, 8, 16], BF16); rkeep = [Res() for _ in range(8)]
                    a2s = Pool([sb(st, f"a2_{i}", [128, TW + 30], BF16) for i in range(3)])
                    diag = sb(st, "diag", [128, CONV_K, 128], BF16); rdiag = Res()
                    FK = sb(st, "FK", [128, 8, TW], F32); rFK = [Res() for _ in range(8)]
                    A = sb(st, "A", [128, 8, TW], BF16); rA = [Res() for _ in range(8)]
                    Y = sb(st, "Y", [128, 8, TW], BF16); rY = [Res() for _ in range(8)]
                    MT = sb(st, "MT", [128, 8, TW], BF16); rMT = [Res() for _ in range(8)]
                    Bp = sb(st, "Bp", [128, 8, 128], F32); rBp = Res()
                    wsb = sb(st, "wsb", [128, 8, 128], BF16); rwsb = Res()
                    st4 = sb(st, "st4", [128, 24], F32); rst4 = Res()
                    wsf = FK[:, 0:2, :].rearrange("p c t -> p (c t)").rearrange("p (g t) -> p g t", g=8)
                    GELv = FK[:].rearrange("p c t -> p (c t)").rearrange("p (k f) -> p k f", k=4)
                    GVv = MT[:].rearrange("p c t -> p (c t)").rearrange("p (k f) -> p k f", k=4)

                    rwsf = rFK[0]
                    op("sp", DMA(wsf, wsT[li]), writes=[rFK[0], rFK[1]], dma=True)
                    op("dve", CP(wsb[:], wsf), reads=[rFK[0], rFK[1]], writes=[rwsb])
                    for hf in range(2):
                        ps, rps = psums.next()
                        op("pe", MM(ps[:], ones_f[:], FK[:, hf, :], True, True),
                           reads=[rFK[hf], r_const], writes=[rps])
                        sbr, rsbr = tmps.next()
                        op("sp", DMA(sbr[0:1, :], sgub[li][:, hf * 512:(hf + 1) * 512]), writes=[rsbr], dma=True)
                        ps2, rps2 = psums.next()
                        op("pe", MM(ps2[:], ones_f[0:1, :], sbr[0:1, :], True, True),
                           reads=[rsbr, r_const], writes=[rps2])
                        for g4 in range(4):
                            g = hf * 4 + g4
                            tb, rtb = tmps.next()
                            op("dve", CP(tb[:, 0:128], ps2[:, g4 * 128:(g4 + 1) * 128]), reads=[rps2], writes=[rtb])
                            op("dve", STT(Bp[:, g, :], ps[:, g4 * 128:(g4 + 1) * 128], pp[:, C_SGUB + g:C_SGUB + g + 1], tb[:, 0:128], ALU.mult, ALU.add),
                               reads=[rps, rtb, r_pp], writes=[rBp])

                    for tt in range(NQ):
                        lo = tt * TW
                        sl = slice(lo, lo + TW)
                        ss, rss = sumsq_bcast([(xT[:, c, sl], [rx[c][tt]]) for c in range(8)])
                        rb, rrb = L0, rL0
                        rstd_bcast(ss[:], rss, 1.0 / D, EPS, rb[:], rrb)
                        for c in range(8):
                            op("dve", STT(hq[:, c, :], xT[:, c, sl], pp[:, C_GPRE + c:C_GPRE + c + 1], rb[:], ALU.mult, ALU.mult),
                               reads=[rx[c][tt], rrb, r_pp], writes=[rhq])
                        has_l = tt > 0
                        has_r = tt < NQ - 1
                        if has_r:
                            tlo, tq, col = lo + TW, tt + 1, 15
                            hsl = slice(tlo, tlo + 15)
                            ssh, rssh = sumsq_bcast([(xT[:, c, hsl], [rx[c][tq]]) for c in range(8)], n=15)
                            rbh, rrbh = tmps.next()
                            rstd_bcast(ssh[:, 0:15], rssh, 1.0 / D, EPS, rbh[:, 0:15], rrbh, n=15)
                            for c in range(8):
                                op("dve", STT(hh[:, c, col:col + 15], xT[:, c, hsl], pp[:, C_GPRE + c:C_GPRE + c + 1], rbh[:, 0:15], ALU.mult, ALU.mult),
                                   reads=[rx[c][tq], rrbh, r_pp], writes=[rhh])

                        def emit_glu(c, tt=tt, has_l=has_l, has_r=has_r):
                            wa, rwa = load_w(w_in[li, c])
                            wb, rwb = load_w(w_in[li, 8 + c])
                            psa, rpsa = psums.next()
                            psb, rpsb = psums.next()
                            for kc in range(8):
                                op("pe", MM(psa[:], wa[:, kc, :], hq[:, kc, :], kc == 0, kc == 7), reads=[rwa, rhq], writes=[rpsa])
                            for kc in range(8):
                                op("pe", MM(psb[:], wb[:, kc, :], hq[:, kc, :], kc == 0, kc == 7), reads=[rwb, rhq], writes=[rpsb])
                            a2, ra2 = a2s.next()
                            th, rth = tmps.next()
                            op("act", ACT(th[:], psb[:], AF.Tanh, scale=0.5), reads=[rpsb], writes=[rth])
                            op("dve", STT(a2[:, 15:15 + TW], th[:], 1.0, psa[:], ALU.add, ALU.mult), reads=[rth, rpsa], writes=[ra2])
                            if has_r:
                                psh, rpsh = psums.next()
                                for (w_, off) in ((wa, 0), (wb, 30)):
                                    for kc in range(8):
                                        op("pe", MM(psh[:, off + 15:off + 30], w_[:, kc, :], hh[:, kc, 15:30], kc == 0, kc == 7),
                                           reads=[rwa, rwb, rhh], writes=[rpsh])
                                thh, rthh = tmps.next()
                                op("act", ACT(thh[:, 15:30], psh[:, 45:60], AF.Tanh, scale=0.5), reads=[rpsh], writes=[rthh])
                                op("dve", STT(a2[:, 15 + TW:30 + TW], thh[:, 15:30], 1.0, psh[:, 15:30], ALU.add, ALU.mult), reads=[rthh, rpsh], writes=[ra2])
                            else:
                                op("dve", MS(a2[:, 15 + TW:30 + TW], 0.0), writes=[ra2])
                            if has_l:
                                op("dve", CP(a2[:, 0:15], keep[:, c, 0:15]), reads=[rkeep[c]], writes=[ra2])
                            else:
                                op("dve", MS(a2[:, 0:15], 0.0), writes=[ra2])
                            if has_r:
                                op("dve", CP(keep[:, c, 0:15], a2[:, TW:TW + 15]), reads=[ra2], writes=[rkeep[c]])
                            return a2, ra2

                        def emit_diag(c):
                            ident_bc = bass.AP(ident_b, 0, [[128, 128], [0, CONV_K], [1, 128]])
                            w_bc = bass.AP(dp, 64 + c * CONV_K, [[320, 128], [1, CONV_K], [0, 128]])
                            op("dve", TT(diag[:], ident_bc, w_bc, ALU.mult), reads=[r_const, r_dp], writes=[rdiag])

                        def emit_conv(c, a2, ra2):
                            psu, rpsu = psums.next()
                            for k in range(CONV_K):
                                op("pe", MM(psu[:], diag[:, k, :], a2[:, k:k + TW], k == 0, k == CONV_K - 1), reads=[rdiag, ra2], writes=[rpsu])
                            op("act", ACT(FK[:, c, :], psu[:], AF.Identity, bias=pp[:, C_CONVB + c:C_CONVB + c + 1]), reads=[rpsu, r_pp], writes=[rFK[c]])

                        for g in range(8):
                            wsv, rwsv = load_w(w_in[li, 48 + g])
                            ps, rps = psums.next()
                            for kt in range(4):
                                for kc in range(8):
                                    op("pe", MM(ps[:, kt * 128:(kt + 1) * 128], hq[:, kc, kt * 128:(kt + 1) * 128], wsv[:, kc, :], kc == 0, kc == 7),
                                       reads=[rwsv, rhq], writes=[rps])
                            op("act", ACT(GELv[:, :, g * 128:(g + 1) * 128], ps[:].rearrange("p (k c) -> p k c", k=4), AF.Gelu),
                               reads=[rps], writes=rFK)
                        nxt = emit_glu(0)
                        emit_diag(0)
                        op("dve", MS(st4[:], 0.0), writes=[rst4])
                        op("dve", RS(st4[:, 0:4], GELv), reads=rFK, writes=[rst4])
                        for kt in range(4):
                            for hf in range(2):
                                jk, rjk = tmps.next()
                                op("act", ACT(jk[:], GELv[:, kt, hf * 512:(hf + 1) * 512], AF.Square, accum_out=st4[:, 16 + kt * 2 + hf:17 + kt * 2 + hf]),
                                   reads=rFK + [rst4], writes=[rjk, rst4])
                        op("dve", RS(st4[:, 4:8], st4[:, 16:24].rearrange("p (k h) -> p k h", h=2)), reads=[rst4], writes=[rst4])
                        op("dve", TS(st4[:, 8:12], st4[:, 0:4], 1.0 / D, None, ALU.mult), reads=[rst4], writes=[rst4])
                        op("dve", TT(st4[:, 12:16], st4[:, 8:12], st4[:, 8:12], ALU.mult), reads=[rst4], writes=[rst4])
                        op("dve", STT(st4[:, 4:8], st4[:, 4:8], 1.0 / D, st4[:, 12:16], ALU.mult, ALU.subtract), reads=[rst4], writes=[rst4])
                        op("act", ACT(st4[:, 12:16], st4[:, 4:8], AF.Ln, bias=epsc[:, 0:1]), reads=[rst4, r_const], writes=[rst4])
                        op("act", ACT(st4[:, 4:8], st4[:, 12:16], AF.Exp, scale=-0.5), reads=[rst4], writes=[rst4])
                        for kt in range(4):
                            op("dve", TS(GVv[:, kt, :], GELv[:, kt, :], st4[:, 8 + kt:9 + kt], st4[:, 4 + kt:5 + kt], ALU.subtract, ALU.mult),
                               reads=rFK + [rst4], writes=rMT)
                        for c in range(8):
                            cur = nxt
                            if c + 1 < 8:
                                nxt = emit_glu(c + 1)
                            emit_conv(c, *cur)
                            if c + 1 < 8:
                                emit_diag(c + 1)
                        ps1, rps1 = psums.next()
                        for c in range(8):
                            op("pe", MM(ps1[:], ones_f[:], FK[:, c, :], c == 0, c == 7), reads=[rFK[c], r_const], writes=[rps1])
                        ps2, rps2 = sumsq_bcast([(FK[:, c, :], [rFK[c]]) for c in range(8)])
                        mean, rmean = L1, rL1
                        op("dve", TS(mean[:], ps1[:], 1.0 / D, None, ALU.mult), reads=[rps1], writes=[rmean])
                        msq, rmsq = tmps.next()
                        op("dve", TT(msq[:], mean[:], mean[:], ALU.mult), reads=[rmean], writes=[rmsq])
                        var, rvar = tmps.next()
                        op("dve", STT(var[:], ps2[:], 1.0 / D, msq[:], ALU.mult, ALU.subtract), reads=[rps2, rmsq], writes=[rvar])
                        rs, rrs = L2, rL2
                        t1_, r1_ = tmps.next()
                        op("act", ACT(t1_[:], var[:], AF.Ln, bias=epsc[:, 0:1]), reads=[rvar, r_const], writes=[r1_])
                        op("act", ACT(rs[:], t1_[:], AF.Exp, scale=-0.5), reads=[r1_], writes=[rrs])
                        for c in range(8):
                            ta, rta = tmps.next()
                            op("dve", TT(ta[:], FK[:, c, :], mean[:], ALU.subtract), reads=[rFK[c], rmean], writes=[rta])
                            tb, rtb = tmps.next()
                            op("dve", TT(tb[:], ta[:], rs[:], ALU.mult), reads=[rta, rrs], writes=[rtb])
                            yh, ryh = tmps.next()
                            op("dve", TS(yh[:], tb[:], dp[:, 24 + c:25 + c], dp[:, 32 + c:33 + c], ALU.mult, ALU.add), reads=[rtb, r_dp], writes=[ryh])
                            th, rth = tmps.next()
                            op("act", ACT(th[:], yh[:], AF.Tanh), reads=[ryh], writes=[rth])
                            op("dve", STT(A[:, c, :], th[:], 1.0, yh[:], ALU.add, ALU.mult), reads=[rth, ryh], writes=[rA[c]])

                        for g in range(8):
                            wsu, rwsu = load_w(w_in[li, 40 + g])
                            psu_, rpsu_ = psums.next()
                            for kc in range(8):
                                op("pe", MM(psu_[:], wsu[:, kc, :], hq[:, kc, :], kc == 0, kc == 7), reads=[rwsu, rhq], writes=[rpsu_])
                            gu, rgu = tmps.next()
                            op("act", ACT(gu[:], psu_[:], AF.Gelu), reads=[rpsu_], writes=[rgu])
                            psm, rpsm = psums.next()
                            for n in range(4):
                                op("pe", MM(psm[:, n * 128:(n + 1) * 128], GVv[:, n, g * 128:(g + 1) * 128], wsb[:, g, :], True, True),
                                   reads=rMT + [rwsb], writes=[rpsm])
                            mx, rmx = tmps.next()
                            bp_bc = bass.AP(Bp, g * 128, [[1024, 128], [0, 4], [1, 128]])
                            op("dve", STT(mx[:].rearrange("p (n t) -> p n t", n=4), psm[:].rearrange("p (n t) -> p n t", n=4),
                                          pp[:, C_SGUG + g:C_SGUG + g + 1], bp_bc, ALU.mult, ALU.add),
                               reads=[rpsm, rBp, r_pp], writes=[rmx])
                            op("dve", TT(Y[:, g, :], gu[:], mx[:], ALU.mult), reads=[rgu, rmx], writes=[rY[g]])

                        for j in range(8):
                            terms = []
                            for (wp_d, src, rsrc, gch, hbcol) in ((w_pa, A, rA, 56 + j, j), (w_pb, None, None, 64 + j, 8 + j), (w_pc, Y, rY, 72 + j, 16 + j)):
                                wp, rwp = load_w(wp_d[li, j])
                                wg, rwg = load_w(w_in[li, gch])
                                pp_, rpp_ = psums.next()
                                pg_, rpg_ = psums.next()
                                for c in range(8):
                                    if src is None:
                                        op("pe", MM(pp_[:], wp[:, c, :], OT[:, c, sl], c == 0, c == 7), reads=[rwp, rOT[c][tt]], writes=[rpp_])
                                    else:
                                        op("pe", MM(pp_[:], wp[:, c, :], src[:, c, :], c == 0, c == 7), reads=[rwp, rsrc[c]], writes=[rpp_])
                                for kc in range(8):
                                    op("pe", MM(pg_[:], wg[:, kc, :], hq[:, kc, :], kc == 0, kc == 7), reads=[rwg, rhq], writes=[rpg_])
                                th, rth = tmps.next()
                                op("act", ACT(th[:], pg_[:], AF.Tanh, bias=dp[:, hbcol:hbcol + 1], scale=0.5), reads=[rpg_, r_dp], writes=[rth])
                                m_, rm_ = tmps.next()
                                op("dve", STT(m_[:], th[:], 1.0, pp_[:], ALU.add, ALU.mult), reads=[rth, rpp_], writes=[rm_])
                                terms.append((m_, rm_))
                            s_, rs_ = tmps.next()
                            op("dve", TT(s_[:], terms[0][0][:], terms[1][0][:], ALU.add), reads=[terms[0][1], terms[1][1]], writes=[rs_])
                            op("dve", TT(MT[:, j, :], s_[:], terms[2][0][:], ALU.add), reads=[rs_, terms[2][1]], writes=[rMT[j]])

                        for i in range(8):
                            wo, rwo = load_w(w_o[li, i])
                            ps, rps = psums.next()
                            for j in range(8):
                                op("pe", MM(ps[:], wo[:, j, :], MT[:, j, :], j == 0, j == 7), reads=[rwo, rMT[j]], writes=[rps])
                            op("act", ACT(FK[:, i, :], ps[:], AF.Copy), reads=[rps], writes=[rFK[i]])
                        ss, rss = sumsq_bcast([(FK[:, i, :], [rFK[i]]) for i in range(8)])
                        rb, rrb = L0, rL0
                        rstd_bcast(ss[:], rss, 1.0 / D, 4.0 * EPS, rb[:], rrb)
                        for i in range(8):
                            ta, rta = tmps.next()
                            op("dve", STT(ta[:], FK[:, i, :], pp[:, C_GPOST + i:C_GPOST + i + 1], rb[:], ALU.mult, ALU.mult), reads=[rFK[i], rrb, r_pp], writes=[rta])
                            op("dve", TT(xT[:, i, sl], xT[:, i, sl], ta[:], ALU.add), reads=[rx[i][tt], rta], writes=[rx[i][tt]])
                P.fence()

            with ExitStack() as st:
                h2 = sb(st, "h2", [128, 8, TW], BF16); rh2 = Res()
                hid = sb(st, "hid", [128, 32, TW], BF16); rhid = [Res() for _ in range(32)]
                Fb = sb(st, "Fb", [128, 8, TW], F32); rFb = [Res() for _ in range(8)]
                wdp = Pool([sb(st, f"wd{i}", [128, 32, 128], BF16) for i in range(2)])
                for tt in range(NQ):
                    sl = slice(tt * TW, (tt + 1) * TW)
                    ss, rss = sumsq_bcast([(xT[:, c, sl], [rx[c][tt]]) for c in range(8)])
                    rb, rrb = L0, rL0
                    rstd_bcast(ss[:], rss, 1.0 / D, EPS, rb[:], rrb)
                    for c in range(8):
                        op("dve", STT(h2[:, c, :], xT[:, c, sl], pp[:, C_FPRE + c:C_FPRE + c + 1], rb[:], ALU.mult, ALU.mult),
                           reads=[rx[c][tt], rrb, r_pp], writes=[rh2])
                    for m in range(32):
                        wu, rwu = load_w(w_up[li, m])
                        ps, rps = psums.next()
                        for kc in range(8):
                            op("pe", MM(ps[:], wu[:, kc, :], h2[:, kc, :], kc == 0, kc == 7), reads=[rwu, rh2], writes=[rps])
                        rl, rrl = tmps.next()
                        op("act", ACT(rl[:], ps[:], AF.Relu), reads=[rps], writes=[rrl])
                        op("dve", TT(hid[:, m, :], rl[:], rl[:], ALU.mult), reads=[rrl], writes=[rhid[m]])
                    for i in range(8):
                        wd, rwd = wdp.next()
                        op("pool", DMA(wd[:], w_dn[li, i]), writes=[rwd], dma=True)
                        ps, rps = psums.next()
                        for m in range(32):
                            op("pe", MM(ps[:], wd[:, m, :], hid[:, m, :], m == 0, m == 31), reads=[rwd, rhid[m]], writes=[rps])
                        op("act", ACT(Fb[:, i, :], ps[:], AF.Copy), reads=[rps], writes=[rFb[i]])
                    ss, rss = sumsq_bcast([(Fb[:, i, :], [rFb[i]]) for i in range(8)])
                    rb, rrb = L0, rL0
                    rstd_bcast(ss[:], rss, 1.0 / D, EPS, rb[:], rrb)
                    for i in range(8):
                        ta, rta = tmps.next()
                        op("dve", STT(ta[:], Fb[:, i, :], pp[:, C_FPOST + i:C_FPOST + i + 1], rb[:], ALU.mult, ALU.mult), reads=[rFb[i], rrb, r_pp], writes=[rta])
                        op("dve", TT(xT[:, i, sl], xT[:, i, sl], ta[:], ALU.add), reads=[rx[i][tt], rta], writes=[rx[i][tt]])
            P.fence()
            if dbg and li == 0:
                for c in range(8):
                    op("sp", DMA(dbg_out[c * 128:(c + 1) * 128, :], xT[:, c, :]), reads=rx[c], dma=True, is_out=True)

        for c in range(8):
            op("sp", DMA(yout[c * 128:(c + 1) * 128, :], xT[:, c, :]), reads=rx[c], dma=True, is_out=True)
        P.emit()
    return nc


def _tile_w(w, kc):
    K, N = w.shape
    return np.ascontiguousarray(w.reshape(kc, 128, N // 128, 128).transpose(2, 1, 0, 3))


def _fm(v):
    return np.ascontiguousarray(v.reshape(-1, 128).T)


def _prep_layer_inputs(inp, layers):
    f = lambda a: np.asarray(a, dtype=np.float32)
    out = {}
    out["w_in"] = np.stack([_tile_w(f(inp["w_in"][l]), 8) for l in layers])
    out["w_pa"] = np.stack([_tile_w(f(inp["w_proj_conv"][l]), 8) for l in layers])
    out["w_pb"] = np.stack([_tile_w(f(inp["w_proj_attn"][l]), 8) for l in layers])
    out["w_pc"] = np.stack([_tile_w(f(inp["w_proj_sgu"][l]), 8) for l in layers])
    out["w_o"] = np.stack([_tile_w(f(inp["w_out"][l]), 8) for l in layers])
    out["w_up"] = np.stack([_tile_w(f(inp["w_ffn_up"][l]), 8) for l in layers])
    out["w_dn"] = np.stack([_tile_w(f(inp["w_ffn_down"][l]), 32) for l in layers])
    out["wsT"] = np.stack([np.ascontiguousarray(f(inp["sgu_w"][l]).transpose(2, 0, 1)) for l in layers])
    out["sgub"] = np.stack([f(inp["sgu_b"][l]).reshape(1, 1024) for l in layers])
    pps = []
    for l in layers:
        p = np.zeros((128, NPP2), np.float32)
        p[:, C_GPRE:C_GPRE + 8] = _fm(f(inp["norm_mix_pre"][l]))
        p[:, C_GPOST:C_GPOST + 8] = _fm(f(inp["norm_mix_post"][l]))
        p[:, C_CONVB:C_CONVB + 8] = _fm(f(inp["conv_b"][l]))
        p[:, C_CLNG:C_CLNG + 8] = _fm(f(inp["conv_ln_g"][l]))
        p[:, C_CLNB:C_CLNB + 8] = _fm(f(inp["conv_ln_b"][l]))
        p[:, C_BGATE:C_BGATE + 24] = _fm(f(inp["b_gate"][l]))
        p[:, C_SUBG] = f(inp["subln_g"][l])
        p[:, C_FPRE:C_FPRE + 8] = _fm(f(inp["norm_ffn_pre"][l]))
        p[:, C_FPOST:C_FPOST + 8] = _fm(f(inp["norm_ffn_post"][l]))
        cw = f(inp["conv_w"][l])
        p[:, C_CONVW:C_CONVW + 248] = cw.T.reshape(8, 128, 31).transpose(1, 0, 2).reshape(128, 248)
        lam = np.concatenate([f(inp[k][l]) for k in ("lam_q1", "lam_k1", "lam_q2", "lam_k2")])
        p[:, C_LAM:C_LAM + 256] = np.broadcast_to(lam[None, :], (128, 256))
        p[:, C_SGUG:C_SGUG + 8] = _fm(f(inp["sgu_ln_g"][l]))
        p[:, C_SGUB:C_SGUB + 8] = _fm(f(inp["sgu_ln_b"][l]))
        pps.append(p)
    out["pp"] = np.stack(pps)
    return out


def _const_tables():
    pidx = np.arange(128, dtype=np.float64)[:, None]
    sstrip = np.abs(np.arange(896, dtype=np.float64)[None, :] - pidx - 384.0).astype(np.float32)
    lin = np.broadcast_to(np.arange(512, dtype=np.float32)[None, :], (128, 512)).copy()
    btab = np.zeros((128, 8, 28), np.float32)
    for h in range(8):
        slope = 2.0 ** (-(h + 1))
        for di in range(28):
            Dd = di * 128 - 1920
            btab[:, h, di] = (-slope * np.abs(Dd - pidx[:, 0])).astype(np.float32)
    return {"ident": np.eye(128, dtype=np.float32), "sstrip": sstrip, "lin": lin,
            "btab": btab.reshape(128, 8 * 28)}


_NC_CACHE = {}


def _run(xT_list, inp, layers):
    key = tuple(layers)
    if key not in _NC_CACHE:
        _NC_CACHE[key] = build_nc(layers)
    nc = _NC_CACHE[key]
    shared = _prep_layer_inputs(inp, layers)
    shared.update(_const_tables())
    in_maps = []
    for b in range(8):
        m = dict(shared)
        m["xT"] = xT_list[b]
        in_maps.append(m)
    res = run_bass_kernel_spmd(nc, in_maps, core_ids=list(range(8)))
    return [np.asarray(r["yT"], dtype=np.float32) for r in res.results]


def kernel(**inputs):
    x = np.asarray(inputs["x"], dtype=np.float32)
    xT = [np.ascontiguousarray(x[b].T) for b in range(8)]
    if FUSED:
        yT = _run(xT, inputs, [0, 1])
    else:
        yT = _run(xT, inputs, [0])
        yT = _run([np.ascontiguousarray(a) for a in yT], inputs, [1])
    return np.stack([a.T for a in yT]).astype(np.float32)
```
